# Optimizing a Trainium2 kernel written in Bass

```python
import math
import jax
import jax.numpy as jnp
from jax import lax
import numpy as np

D_MODEL = 1024
BATCH = 16
SEQ = 256
DEPTH = 4
DEC_BATCH = 4
DEC_SEQ = 1024
PAST_LEN = 512

GRID_W = 64
N_MIXERS = 3
N_ATTN = len(range(0, DEPTH, N_MIXERS))
N_REC = len(range(1, DEPTH, N_MIXERS))
N_FOUR = len(range(2, DEPTH, N_MIXERS))

ATTN_HEADS = 8
ATTN_DH = D_MODEL // ATTN_HEADS // 2
ATTN_DV = 2 * ATTN_DH
ROPE_THETA = 10000.0
Q_BLOCK = 128

REC_HEADS = 8
REC_DK = D_MODEL // REC_HEADS
REC_DV = D_MODEL // REC_HEADS
REC_CHUNK = 16

FOUR_GROUPS = 4
FOUR_DG = D_MODEL // FOUR_GROUPS

D_FF = ((8 * D_MODEL // 3 + 255) // 256) * 256
N_MOD = 6
EPS = 1e-6

kernel_name = 'hybrid_diffattn_hgrn2_fnet_prefix_dit_step'

F32 = jnp.float32


def rmsnorm(x, g):
    xf = x.astype(F32)
    y = xf * lax.rsqrt(jnp.mean(xf * xf, axis=-1, keepdims=True) + EPS)
    return (y * g.astype(F32)).astype(x.dtype)


def modulation(cond, w, b):
    m = jax.nn.silu(cond.reshape(-1, D_MODEL)) @ w + b
    return [t[:, None, :] for t in jnp.split(m, N_MOD, axis=-1)]


def adanorm(x, g, shift, scale):
    return rmsnorm(x, g) * (1.0 + scale) + shift


def grid_positions(n_tokens):
    rows = n_tokens // GRID_W
    row = jnp.repeat(jnp.arange(rows), GRID_W)
    col = jnp.tile(jnp.arange(GRID_W), rows)
    return row, col


def rope_axis(x, pos):
    half = x.shape[-1] // 2
    inv = ROPE_THETA ** (-jnp.arange(half, dtype=F32) / half)
    ang = pos.astype(F32)[:, None] * inv
    cos = jnp.cos(ang)[None, :, None, None, :]
    sin = jnp.sin(ang)[None, :, None, None, :]
    x1, x2 = x[..., :half], x[..., half:]
    return jnp.concatenate([x1 * cos - x2 * sin, x2 * cos + x1 * sin], axis=-1)


def rope_2d(x, row, col):
    xf = x.astype(F32)
    n = ATTN_DH // 2
    out = jnp.concatenate([rope_axis(xf[..., :n], row), rope_axis(xf[..., n:], col)], axis=-1)
    return out.astype(x.dtype)


def diff_attn_core(q, k, v, lam):
    bsz, lq = q.shape[0], q.shape[1]
    nb = lq // Q_BLOCK
    qb = q.reshape(bsz, nb, Q_BLOCK, ATTN_HEADS, 2, ATTN_DH).swapaxes(0, 1)
    scale = ATTN_DH ** -0.5
    vf = v.astype(F32)

    def one_block(qi):
        s = jnp.einsum('bqhcd,bkhcd->bhcqk', qi, k).astype(F32) * scale
        p = jax.nn.softmax(s, axis=-1)
        pd = p[:, :, 0] - lam * p[:, :, 1]
        return jnp.einsum('bhqk,bkhe->bqhe', pd, vf)

    o = lax.map(one_block, qb)
    return o.swapaxes(0, 1).reshape(bsz, lq, ATTN_HEADS, ATTN_DV)


def attn_project(h, w_qkv):
    bsz, n = h.shape[0], h.shape[1]
    q, k, v = jnp.split(h @ w_qkv, 3, axis=-1)
    q = q.reshape(bsz, n, ATTN_HEADS, 2, ATTN_DH)
    k = k.reshape(bsz, n, ATTN_HEADS, 2, ATTN_DH)
    v = v.reshape(bsz, n, ATTN_HEADS, ATTN_DV)
    return q, k, v


def attn_output(o, g_subln, lam_init, w_o, dtype):
    bsz, n = o.shape[0], o.shape[1]
    o = rmsnorm(o, g_subln) * (1.0 - lam_init)
    return o.reshape(bsz, n, D_MODEL).astype(dtype) @ w_o


def chunk_gla(q, k, v, lf, s0):
    bsz, n, nh, dk = q.shape
    dv = v.shape[-1]
    c = REC_CHUNK
    nc = n // c
    q = q.reshape(bsz, nc, c, nh, dk)
    k = k.reshape(bsz, nc, c, nh, dk)
    v = v.reshape(bsz, nc, c, nh, dv)
    b = jnp.cumsum(lf.reshape(bsz, nc, c, nh, dk), axis=2)
    mask = jnp.tril(jnp.ones((c, c), dtype=bool))[:, :, None, None]
    diff = b[:, :, :, None] - b[:, :, None, :]
    dec = jnp.where(mask, jnp.exp(jnp.where(mask, diff, 0.0)), 0.0)
    a = jnp.sum(q[:, :, :, None] * k[:, :, None, :] * dec, axis=-1)
    o_intra = jnp.einsum('bntsh,bnshv->bnthv', a, v)
    b_last = b[:, :, -1]
    kv = jnp.einsum('bnshk,bnshv->bnhkv', k * jnp.exp(b_last[:, :, None] - b), v)

    def step(s, inp):
        d, kv_c = inp
        return d[..., None] * s + kv_c, s

    s_fin, s_start = lax.scan(step, s0, (jnp.exp(b_last).swapaxes(0, 1), kv.swapaxes(0, 1)))
    o_inter = jnp.einsum('bnthk,nbhkv->bnthv', q * jnp.exp(b), s_start)
    return (o_intra + o_inter).reshape(bsz, n, nh, dv), s_fin


def hgrn_lower_bounds(lb_logits, layer):
    s = jax.nn.softmax(lb_logits.astype(F32), axis=1)
    lb = jnp.cumsum(s, axis=1) - s[:, :1]
    return lb[:, layer]


def hgrn_project(h, w_in, lb):
    bsz, n = h.shape[0], h.shape[1]
    q, vi, g, zf, zb = jnp.split(h @ w_in, 5, axis=-1)
    q = jax.nn.silu(q.astype(F32)).reshape(bsz, n, REC_HEADS, REC_DK)
    vi = vi.astype(F32).reshape(bsz, n, REC_HEADS, REC_DV)

    def gates(z, lbd):
        f = lbd + (1.0 - lbd) * jax.nn.sigmoid(z.astype(F32))
        return (1.0 - f).reshape(bsz, n, REC_HEADS, REC_DK), jnp.log(f).reshape(bsz, n, REC_HEADS, REC_DK)

    kf, lff = gates(zf, lb[0])
    kb, lfb = gates(zb, lb[1])
    return q, vi, g, kf, lff, kb, lfb


def hgrn_mix(q, vi, kf, lff, kb, lfb, s0f, s0b):
    of, sf = chunk_gla(q, kf, vi, lff, s0f)
    ob, sb = chunk_gla(q[:, ::-1], kb[:, ::-1], vi[:, ::-1], lfb[:, ::-1], s0b)
    return of + ob[:, ::-1], jnp.stack([sf, sb], axis=1)


def hgrn_output(o, g, g_out, w_o, dtype):
    bsz, n = o.shape[0], o.shape[1]
    gate = jax.nn.silu(g.astype(F32)).reshape(bsz, n, REC_HEADS, REC_DV)
    o = rmsnorm(o, g_out) * gate
    return o.reshape(bsz, n, D_MODEL).astype(dtype) @ w_o


def fourier_mix(h, w):
    bsz, n = h.shape[0], h.shape[1]
    hg = h.astype(F32).reshape(bsz, n, FOUR_GROUPS, FOUR_DG)
    f = jnp.fft.fftn(hg, axes=(1, 3), norm='ortho').real
    return f.reshape(bsz, n, D_MODEL).astype(h.dtype) @ w


def swiglu(h, w_in, w_out):
    gt, up = jnp.split(h @ w_in, 2, axis=-1)
    return (jax.nn.silu(gt) * up) @ w_out


def setup_inputs(seed: int = 0) -> dict:
    key = jax.random.key(seed)
    ks = jax.random.split(key, 24)
    nrm = jax.random.normal
    d = D_MODEL
    return {
        'x_prompt': nrm(ks[0], (BATCH, SEQ, d), F32),
        'x_sample': nrm(ks[1], (DEC_BATCH, DEC_SEQ, d), F32),
        'cache_attn_k': nrm(ks[2], (DEC_BATCH, N_ATTN, PAST_LEN, ATTN_HEADS, 2, ATTN_DH), F32),
        'cache_attn_v': nrm(ks[3], (DEC_BATCH, N_ATTN, PAST_LEN, ATTN_HEADS, ATTN_DV), F32),
        'state_hgrn': 0.5 * nrm(ks[4], (DEC_BATCH, N_REC, 2, REC_HEADS, REC_DK, REC_DV), F32),
        'c': nrm(ks[5], (DEC_BATCH, d), F32),
        'c_ctx': nrm(ks[6], (d,), F32),
        'w_ada': nrm(ks[7], (DEPTH, d, N_MOD * d), F32) * d ** -0.5,
        'b_ada': 0.02 * nrm(ks[8], (DEPTH, N_MOD * d), F32),
        'g_norm_mix': 1.0 + 0.02 * nrm(ks[9], (DEPTH, d), F32),
        'g_norm_ffn': 1.0 + 0.02 * nrm(ks[10], (DEPTH, d), F32),
        'w_qkv_attn': nrm(ks[11], (N_ATTN, d, 3 * d), F32) * d ** -0.5,
        'lam_attn': 0.1 * nrm(ks[12], (N_ATTN, 4, ATTN_DH), F32),
        'g_subln_attn': 1.0 + 0.02 * nrm(ks[13], (N_ATTN, ATTN_DV), F32),
        'w_o_attn': nrm(ks[14], (N_ATTN, d, d), F32) * d ** -0.5,
        'w_in_rec': nrm(ks[15], (N_REC, d, 5 * d), F32) * d ** -0.5,
        'lb_logits_rec': 0.5 * nrm(ks[16], (2, DEPTH, REC_HEADS * REC_DK), F32),
        'g_out_rec': 1.0 + 0.02 * nrm(ks[17], (N_REC, REC_DV), F32),
        'w_o_rec': nrm(ks[18], (N_REC, d, d), F32) * d ** -0.5,
        'w_four': nrm(ks[19], (N_FOUR, d, d), F32) * d ** -0.5,
        'w_ffn_in': nrm(ks[20], (DEPTH, d, 2 * D_FF), F32) * d ** -0.5,
        'w_ffn_out': nrm(ks[21], (DEPTH, D_FF, d), F32) * D_FF ** -0.5,
        'g_final': 1.0 + 0.02 * nrm(ks[22], (d,), F32),
    }


def reference(x_prompt, x_sample, cache_attn_k, cache_attn_v, state_hgrn, c, c_ctx,
              w_ada, b_ada, g_norm_mix, g_norm_ffn, w_qkv_attn, lam_attn, g_subln_attn, w_o_attn,
              w_in_rec, lb_logits_rec, g_out_rec, w_o_rec, w_four, w_ffn_in, w_ffn_out, g_final):
    n_lat = x_sample.shape[1]
    row, col = grid_positions(n_lat)
    xp, xs = x_prompt, x_sample
    new_k, new_v, new_s = [], [], []
    for i in range(DEPTH):
        kind, j = i % N_MIXERS, i // N_MIXERS
        mp = modulation(c_ctx, w_ada[i], b_ada[i])
        ms = modulation(c, w_ada[i], b_ada[i])
        hp = adanorm(xp, g_norm_mix[i], mp[0], mp[1])
        hs = adanorm(xs, g_norm_mix[i], ms[0], ms[1])
        if kind == 0:
            lam_init = 0.8 - 0.6 * math.exp(-0.3 * i)
            lp = lam_attn[j].astype(F32)
            lam = jnp.exp(jnp.sum(lp[0] * lp[1])) - jnp.exp(jnp.sum(lp[2] * lp[3])) + lam_init
            qp, kp, vp = attn_project(hp, w_qkv_attn[j])
            op = attn_output(diff_attn_core(qp, kp, vp, lam), g_subln_attn[j], lam_init, w_o_attn[j], xp.dtype)
            new_k.append(kp)
            new_v.append(vp)
            qs, ks_, vs = attn_project(hs, w_qkv_attn[j])
            qs, ks_ = rope_2d(qs, row, col), rope_2d(ks_, row, col)
            k_all = jnp.concatenate([cache_attn_k[:, j].astype(ks_.dtype), ks_], axis=1)
            v_all = jnp.concatenate([cache_attn_v[:, j].astype(vs.dtype), vs], axis=1)
            os_ = attn_output(diff_attn_core(qs, k_all, v_all, lam), g_subln_attn[j], lam_init, w_o_attn[j], xs.dtype)
        elif kind == 1:
            lb = hgrn_lower_bounds(lb_logits_rec, i)
            q, vi, g, kf, lff, kb, lfb = hgrn_project(hp, w_in_rec[j], lb)
            zero = jnp.zeros((xp.shape[0], REC_HEADS, REC_DK, REC_DV), F32)
            o, s_ctx = hgrn_mix(q, vi, kf, lff, kb, lfb, zero, zero)
            op = hgrn_output(o, g, g_out_rec[j], w_o_rec[j], xp.dtype)
            new_s.append(s_ctx.astype(xp.dtype))
            q, vi, g, kf, lff, kb, lfb = hgrn_project(hs, w_in_rec[j], lb)
            st = state_hgrn[:, j].astype(F32)
            o, _ = hgrn_mix(q, vi, kf, lff, kb, lfb, st[:, 0], st[:, 1])
            os_ = hgrn_output(o, g, g_out_rec[j], w_o_rec[j], xs.dtype)
        else:
            op = fourier_mix(hp, w_four[j])
            os_ = fourier_mix(hs, w_four[j])
        xp = xp + mp[2] * op
        xs = xs + ms[2] * os_
        xp = xp + mp[5] * swiglu(adanorm(xp, g_norm_ffn[i], mp[3], mp[4]), w_ffn_in[i], w_ffn_out[i])
        xs = xs + ms[5] * swiglu(adanorm(xs, g_norm_ffn[i], ms[3], ms[4]), w_ffn_in[i], w_ffn_out[i])
    y_prompt = rmsnorm(xp, g_final)
    y_sample = rmsnorm(xs, g_final)
    return (y_prompt, y_sample, jnp.stack(new_k, axis=1), jnp.stack(new_v, axis=1), jnp.stack(new_s, axis=1))
```

```python
import contextlib
import math
import os
import numpy as np
import concourse.bass as bass
import concourse.mybir as mybir
from concourse.bass_utils import run_bass_kernel_spmd

F32 = mybir.dt.float32
BF16 = mybir.dt.bfloat16
AF = mybir.ActivationFunctionType
ALU = mybir.AluOpType
AX = mybir.AxisListType

D = 1024
DEPTH = 4
DFF = 2816
NTOK = 1024
EPS = 1e-6
NB = 3
SKIP = set(os.environ.get('KSKIP', '').split(','))
NCORES = int(os.environ.get('KCORES', 8))
SLOT = 5120

C_COND = 0
C_GMIX = C_COND + 8
C_GFFN = C_GMIX + 32
C_BADA = C_GFFN + 32
C_GFIN = C_BADA + 192
C_GSUB = C_GFIN + 8
C_GOUT = C_GSUB + 2
C_LBL = C_GOUT + 1
C_MASK = C_LBL + 64
C_KEEP = C_MASK + 48
C_LAM = C_KEEP + 1
NV = C_LAM + 512


class Tracker:
    def __init__(self, nc, es):
        self.nc = nc
        self.es = es
        self.eng = {'pe': nc.tensor, 'act': nc.scalar, 'dve': nc.vector, 'pool': nc.gpsimd, 'sp': nc.sync}
        self.sems = {}
        self.count = {}
        self.waited = {}
        self.lastw = {}
        self.readers = {}
        self.muted = False

    def sem(self, key):
        if key not in self.sems:
            name = "s_" + "_".join(str(k) for k in (key if isinstance(key, tuple) else (key,)))
            self.sems[key] = self.es.enter_context(self.nc.semaphore(name))
            self.count[key] = 0
        return self.sems[key]

    def _deps(self, reads, writes):
        deps = {}
        def add(sk, v):
            if v > deps.get(sk, 0):
                deps[sk] = v
        for r in reads:
            if r in self.lastw:
                add(*self.lastw[r])
        for w in writes:
            if w in self.lastw:
                add(*self.lastw[w])
            for sk, v in self.readers.get(w, {}).items():
                add(sk, v)
        return deps

    def _wait(self, e, deps):
        for sk, v in deps.items():
            if sk == 'pe' and e == 'pe':
                continue
            if self.waited.get((e, sk), 0) >= v:
                continue
            self.eng[e].wait_ge(self.sems[sk], v)
            self.waited[(e, sk)] = v

    def _record(self, sk, tick, reads, writes):
        for w in writes:
            self.lastw[w] = (sk, tick)
            self.readers[w] = {}
        for r in reads:
            d = self.readers.setdefault(r, {})
            if tick > d.get(sk, 0):
                d[sk] = tick

    def op(self, e, fn, reads=(), writes=(), signal=True):
        if self.muted:
            return None
        self.sem(e)
        self._wait(e, self._deps(reads, writes))
        inst = fn(self.eng[e])
        if signal:
            self.count[e] += 1
            inst.then_inc(self.sems[e], 1)
            tick = self.count[e]
        else:
            tick = self.count[e] + 1
        self._record(e, tick, reads, writes)
        return inst

    def dma(self, q, out, in_, reads, writes, semkey):
        if self.muted:
            return None
        self.sem(semkey)
        self._wait(q, self._deps(reads, writes))
        self.eng[q].dma_start(out=out, in_=in_).then_inc(self.sems[semkey], 16)
        self.count[semkey] += 16
        self._record(semkey, self.count[semkey], reads, writes)

    def barrier(self):
        for e in ('pe', 'act', 'dve', 'pool', 'sp'):
            for sk in ('pe', 'act', 'dve', 'pool'):
                if sk in self.sems and sk != e:
                    v = self.count[sk]
                    if v > self.waited.get((e, sk), 0):
                        self.eng[e].wait_ge(self.sems[sk], v)
                        self.waited[(e, sk)] = v
            for sk in list(self.sems.keys()):
                if sk in ('pe', 'act', 'dve', 'pool'):
                    continue
                v = self.count[sk]
                if v > self.waited.get((e, sk), 0):
                    self.eng[e].wait_ge(self.sems[sk], v)
                    self.waited[(e, sk)] = v


def build_program(depth_run=DEPTH):
    nc = bass.Bass("TRN2", target_bir_lowering=False)
    dram = lambda name, shape, kind: nc.dram_tensor(name, shape, F32, kind=kind).ap()
    IN, OUT = "ExternalInput", "ExternalOutput"
    x_in = dram("xT", [D, NTOK], IN)
    kc_in = dram("kcT", [2, D, 512], IN)
    vc_in = dram("vc", [2, 512, D], IN)
    s0_in = dram("s0", [8, 128, 2, 128], IN)
    vecs_in = dram("vecs", [128, NV], IN)
    rope_in = dram("rope", [2, 128, NTOK], IN)
    cst_in = dram("cst", [2, 128, 128], IN)
    hmask_in = dram("hmask", [128, 2, 128], IN)
    cmask_in = dram("cmask", [128, 8, 128], IN)
    dftn_in = dram("dftn", [2, NTOK, NTOK], IN)
    dftc_in = dram("dftc", [256, 512], IN)
    w_ada = dram("w_ada", [DEPTH, D, 6 * D], IN)
    w_qkv = dram("w_qkv", [2, D, 3 * D], IN)
    w_oa = dram("w_oa", [2, D, D], IN)
    w_rin = dram("w_rin", [1, D, 5 * D], IN)
    w_or = dram("w_or", [1, D, D], IN)
    w_four = dram("w_four", [1, D, D], IN)
    w_fin = dram("w_fin", [DEPTH, D, 2 * DFF], IN)
    w_fout = dram("w_fout", [DEPTH, DFF, D], IN)
    y_out = dram("yT", [D, NTOK], OUT)
    k_out = dram("koutT", [2, D, NTOK], OUT)
    v_out = dram("vout", [2, NTOK, D], OUT)
    s_out = dram("sout", [4, 2, 8, 128, 128], OUT)

    kp = lambda w: w.rearrange("(kc p) n -> p kc n", p=128)

    pieces = []
    def v3(n_k, n_c, off=0):
        return lambda sl: sl[:, off:off + n_k * n_c].rearrange("p (k c) -> p k c", c=n_c)
    ada_piece = lambda i, p: ('ada', [(v3(8, 512), kp(w_ada[i])[:, :, p * 512:(p + 1) * 512])])
    for p in range(4):
        pieces.append(ada_piece(0, p))
    for i in range(depth_run):
        kind, j = i % 3, i // 3
        if kind == 0:
            qkv_piece = lambda p: ('qkv', [(v3(8, 512), kp(w_qkv[j])[:, :, p * 512:(p + 1) * 512])])
            if i == 0:
                order = ['q0', 'a4', 'q1', 'a5', 'q2', 'a6', 'q3', 'a7', 'q4', 'a8', 'q5', 'a9', 'a10', 'a11']
            else:
                order = ['q0', 'q1', 'q2', 'q3', 'q4', 'q5']
            for o in order:
                pieces.append(qkv_piece(int(o[1:])) if o[0] == 'q' else ada_piece(0, int(o[1:])))
            wo = w_oa[j]
        elif kind == 1:
            for h in range(8):
                pieces.append(('rec', [(v3(8, 128, g * 1024), kp(w_rin[j])[:, :, g * 1024 + h * 128: g * 1024 + (h + 1) * 128]) for g in range(5)]))
            wo = w_or[j]
        else:
            wo = w_four[j]
        for p in range(2):
            pieces.append(('wo', [(v3(8, 512), kp(wo)[:, :, p * 512:(p + 1) * 512])]))
        for p in range(11):
            pieces.append(('ffin', [(v3(8, 256, u * 2048), kp(w_fin[i])[:, :, u * DFF + p * 256: u * DFF + (p + 1) * 256]) for u in range(2)]))
            if i + 1 < depth_run and p < 7:
                pieces.append(ada_piece(i + 1, p))
        for c in range(8):
            pieces.append(('ffout', [(v3(22, 128), kp(w_fout[i])[:, :, c * 128:(c + 1) * 128])]))
            if i + 1 < depth_run and c < 5:
                pieces.append(ada_piece(i + 1, 7 + c))

    with contextlib.ExitStack() as es:
        T = Tracker(nc, es)
        _uid = [0]
        def sb(name, shape, dt, st=es):
            _uid[0] += 1
            return st.enter_context(nc.sbuf_tensor(f"{name}_{_uid[0]}", shape, dt))
        xT = sb("xTs", [128, 8, NTOK], F32)
        hT = sb("hT", [128, 8, NTOK], BF16)
        wsl = sb("wsl", [128, NB, SLOT], BF16)
        vecs = sb("vecs_s", [128, NV], F32)
        modv = sb("modv", [128, 48], F32)
        A1 = sb("A1", [128, 8], F32)
        A2 = sb("A2", [128, 8], F32)
        scT = sb("scT", [128, 8], BF16)
        onesM = sb("onesM", [128, 128], BF16)
        ones128 = sb("ones128", [128, 128], BF16)
        ones1 = sb("ones1", [128, 128], BF16)
        zerosW = sb("zerosW", [128, 128], BF16)
        cstb = sb("cstb", [128, 2, 128], BF16)
        sq = sb("sq", [128, 8, 512], BF16)
        lnt = sb("lnt", [128, 512], F32)
        rstd = sb("rstd", [128, 512], F32)
        tmpA = sb("tmpA", [128, 2, 512], F32)
        small = sb("small", [128, 16], F32)
        ps = es.enter_context(nc.psum_tensor("ps", [128, 7, 512], F32))
        pbf = es.enter_context(nc.psum_tensor("pbf", [128, 1024], BF16))
        PS = lambda b: ps[:, b, :]

        ws = {'issued': 0, 'next': 0}
        def ws_issue_upto(n):
            while ws['issued'] <= min(n, len(pieces) - 1):
                m = ws['issued']
                s = m % NB
                for (dst_fn, src) in pieces[m][1]:
                    T.dma('pool', dst_fn(wsl[:, s, :]), src, reads=[], writes=[('w', s)], semkey=('w', s))
                ws['issued'] += 1
        def ws_get(kind):
            n = ws['next']
            assert pieces[n][0] == kind, (pieces[n][0], kind)
            ws_issue_upto(n + NB - 1)
            ws['next'] += 1
            s = n % NB
            return s, wsl[:, s, :]

        half = lambda th: slice(th * 512, (th + 1) * 512)

        T.dma('sp', vecs[:], vecs_in, [], ['vecs'], 'ld_vecs')
        for k in range(8):
            T.dma('sp', xT[:, k, :], x_in[k * 128:(k + 1) * 128, :], [], [('xT', 0), ('xT', 1)], 'ld_x')
        T.dma('pool', cstb[:], cst_in.rearrange("a p n -> p a n"), [], ['cstb'], 'ld_cst')
        T.op('dve', lambda e: e.memset(onesM[:], 1.0 / 1024.0), [], ['onesM'])
        T.op('dve', lambda e: e.memset(ones128[:], 1.0 / 128.0), [], ['ones128'])
        T.op('dve', lambda e: e.memset(ones1[:], 1.0), [], ['ones1'])
        T.op('dve', lambda e: e.memset(zerosW[:], 0.0), [], ['zerosW'])
        T.op('act', lambda e: e.activation(out=scT[:], in_=vecs[:, C_COND:C_COND + 8], func=AF.Silu), ['vecs'], ['scT'])
        ws_issue_upto(NB - 1)

        tmp_i = [0]
        def norm_apply(th, Acol, Bcol, out_fn):
            T.op('act', lambda e: e.activation(out=sq[:], in_=xT[:, :, half(th)], func=AF.Square), [('xT', th)], ['sq'])
            for k in range(8):
                T.op('pe', lambda e: e.matmul(PS(0), onesM[:], sq[:, k, :], start=(k == 0), stop=(k == 7)),
                     ['onesM', 'sq'], [('ps', 0)], signal=(k == 7))
            T.op('act', lambda e: e.activation(out=lnt[:], in_=PS(0), func=AF.Ln, bias=EPS, scale=1.0), [('ps', 0)], ['lnt'])
            T.op('act', lambda e: e.activation(out=rstd[:], in_=lnt[:], func=AF.Exp, scale=-0.5), ['lnt'], ['rstd'])
            for k in range(8):
                ti = tmp_i[0] % 2
                tmp_i[0] += 1
                T.op('dve', lambda e: e.scalar_tensor_tensor(out=tmpA[:, ti, :], in0=xT[:, k, half(th)], scalar=Acol[:, k:k + 1],
                                                             in1=rstd[:], op0=ALU.mult, op1=ALU.mult),
                     [('xT', th), 'rstd', 'mod'], [('tmpA', ti)])
                out_fn(k, ti)

        def adanorm(Acol, Bcol):
            for th in range(2):
                def out_fn(k, ti, th=th):
                    T.op('act', lambda e: e.activation(out=hT[:, k, half(th)], in_=tmpA[:, ti, :], func=AF.Identity,
                                                       bias=Bcol[:, k:k + 1], scale=1.0),
                         [('tmpA', ti), 'mod'], [('hT', th)])
                norm_apply(th, Acol, Bcol, out_fn)

        def mod_piece(p):
            s, sl = ws_get('ada')
            w = sl[:, 0:4096].rearrange("p (k c) -> p k c", c=512)
            for c4 in range(4):
                col = p * 4 + c4
                for k in range(8):
                    T.op('pe', lambda e: e.matmul(ps[:, 1, col:col + 1], w[:, k, c4 * 128:(c4 + 1) * 128], scT[:, k:k + 1],
                                                  start=(k == 0), stop=(k == 7)),
                         [('w', s), 'scT'], [('ps', 1)], signal=(k == 7 and c4 == 3))

        def mod_finish(i, part='all'):
            lo, hi = {'all': (0, 48), 'A': (0, 16), 'B': (16, 48)}[part]
            T.op('dve', lambda e: e.tensor_tensor(out=modv[:, lo:hi], in0=ps[:, 1, lo:hi], in1=vecs[:, C_BADA + i * 48 + lo:C_BADA + i * 48 + hi], op=ALU.add),
                 [('ps', 1), 'vecs'], ['mod'])
            if part in ('all', 'A'):
                T.op('dve', lambda e: e.scalar_tensor_tensor(out=A1[:], in0=modv[:, 8:16], scalar=1.0, in1=vecs[:, C_GMIX + i * 8:C_GMIX + (i + 1) * 8],
                                                             op0=ALU.add, op1=ALU.mult), ['mod', 'vecs'], ['mod'])
            if part in ('all', 'B'):
                T.op('dve', lambda e: e.scalar_tensor_tensor(out=A2[:], in0=modv[:, 32:40], scalar=1.0, in1=vecs[:, C_GFFN + i * 8:C_GFFN + (i + 1) * 8],
                                                             op0=ALU.add, op1=ALU.mult), ['mod', 'vecs'], ['mod'])

        def out_proj_residual(src, gate_off, src_key):
            bank = 0
            for p in range(2):
                s, sl = ws_get('wo')
                w = sl[:, 0:4096].rearrange("p (k c) -> p k c", c=512)
                for c4 in range(4):
                    fc = p * 4 + c4
                    for th in range(2):
                        b = 2 + (bank % 2)
                        bank += 1
                        for k in range(8):
                            T.op('pe', lambda e: e.matmul(PS(b), w[:, k, c4 * 128:(c4 + 1) * 128], src[:, k, half(th)], start=(k == 0), stop=(k == 7)),
                                 [('w', s), (src_key, th)], [('ps', b)], signal=(k == 7))
                        T.op('dve', lambda e: e.scalar_tensor_tensor(out=xT[:, fc, half(th)], in0=PS(b), scalar=modv[:, gate_off + fc:gate_off + fc + 1],
                                                                     in1=xT[:, fc, half(th)], op0=ALU.mult, op1=ALU.add),
                             [('ps', b), 'mod', ('xT', th)], [('xT', th)])

        def ffn(i):
            with contextlib.ExitStack() as ph:
                HT = sb("HT", [128, 22, NTOK], BF16, ph)
                sg = sb("sg", [128, 2, 512], F32, ph)
                adanorm(A2, modv[:, 24:32])
                cnt = 0
                for p in range(11):
                    s, sl = ws_get('ffin')
                    wg = sl[:, 0:2048].rearrange("p (k c) -> p k c", c=256)
                    wu = sl[:, 2048:4096].rearrange("p (k c) -> p k c", c=256)
                    for cc in range(2):
                        ff = p * 2 + cc
                        for th in range(2):
                            bg, bu = 2 + 2 * (cnt % 2), 3 + 2 * (cnt % 2)
                            si = cnt % 2
                            cnt += 1
                            for k in range(8):
                                T.op('pe', lambda e: e.matmul(PS(bg), wg[:, k, cc * 128:(cc + 1) * 128], hT[:, k, half(th)], start=(k == 0), stop=(k == 7)),
                                     [('w', s), ('hT', th)], [('ps', bg)], signal=(k == 7))
                            for k in range(8):
                                T.op('pe', lambda e: e.matmul(PS(bu), wu[:, k, cc * 128:(cc + 1) * 128], hT[:, k, half(th)], start=(k == 0), stop=(k == 7)),
                                     [('w', s), ('hT', th)], [('ps', bu)], signal=(k == 7))
                            T.op('act', lambda e: e.activation(out=sg[:, si, :], in_=PS(bg), func=AF.Silu), [('ps', bg)], [('sg', si)])
                            T.op('dve', lambda e: e.tensor_tensor(out=HT[:, ff, half(th)], in0=PS(bu), in1=sg[:, si, :], op=ALU.mult),
                                 [('ps', bu), ('sg', si)], [('HT', th)])
                    if i + 1 < depth_run and p < 7:
                        mod_piece(p)
                bank = 0
                for fc in range(8):
                    s, sl = ws_get('ffout')
                    w = sl[:, 0:2816].rearrange("p (k c) -> p k c", c=128)
                    for th in range(2):
                        b = 2 + (bank % 2)
                        bank += 1
                        for k in range(22):
                            T.op('pe', lambda e: e.matmul(PS(b), w[:, k, :], HT[:, k, half(th)], start=(k == 0), stop=(k == 21)),
                                 [('w', s), ('HT', th)], [('ps', b)], signal=(k == 21))
                        T.op('dve', lambda e: e.scalar_tensor_tensor(out=xT[:, fc, half(th)], in0=PS(b), scalar=modv[:, 40 + fc:41 + fc],
                                                                     in1=xT[:, fc, half(th)], op0=ALU.mult, op1=ALU.add),
                             [('ps', b), 'mod', ('xT', th)], [('xT', th)])
                    if i + 1 < depth_run and fc < 5:
                        mod_piece(7 + fc)
                T.barrier()

        def attention(i, j):
            lam_init = 0.8 - 0.6 * math.exp(-0.3 * i)
            with contextlib.ExitStack() as ph:
                QT = sb("QT", [128, 8, NTOK], BF16, ph)
                KT = sb("KT", [128, 8, 1536], BF16, ph)
                V = sb("Vtm", [128, 12, D], BF16, ph)
                Pt = sb("Pt", [128, 3, 512], BF16, ph)
                Qp = sb("Qp", [128, 2, 512], BF16, ph)
                qraw = sb("qraw", [128, 2, 512], BF16, ph)
                ropet = sb("ropet", [128, 2, NTOK], F32, ph)
                kst = sb("kst", [128, 2, 512], F32, ph)
                vst = sb("vst", [128, 2, 512], F32, ph)
                t1 = sb("t1", [128, 1, 512], F32, ph)
                rr = sb("rr", [128, 512], F32, ph)
                tt = sb("tt", [128, 512], F32, ph)
                Oh = sb("Oh", [128, 1, NTOK], F32, ph)
                sqo = sb("sqo", [128, 2, 512], BF16, ph)
                lamt = sb("lamt", [128, 2, 64], F32, ph)
                T.muted = 'aload' in SKIP
                T.dma('sp', ropet[:], rope_in.rearrange("a p n -> p a n"), [], ['ropet'], 'ld_rope')
                T.dma('pool', KT[:, :, 0:512], kc_in[j].rearrange("(h p) n -> p h n", p=128), [], ['KTc'], 'ld_kc')
                T.dma('pool', V[:, 0:4, :], vc_in[j].rearrange("(kb p) f -> p kb f", p=128), [], ['Vc'], 'ld_vc')
                T.muted = 'lam' in SKIP
                lv = vecs[:, C_LAM + j * 256:C_LAM + (j + 1) * 256].rearrange("p (a b d) -> p a b d", a=2, b=2)
                T.op('dve', lambda e: e.tensor_tensor(out=lamt[:], in0=lv[:, :, 0, :], in1=lv[:, :, 1, :], op=ALU.mult), ['vecs'], ['lamt'])
                T.op('dve', lambda e: e.tensor_reduce(out=small[:, 0:2], in_=lamt[:], axis=AX.X, op=ALU.add), ['lamt'], ['small'])
                T.op('act', lambda e: e.activation(out=small[:, 2:4], in_=small[:, 0:2], func=AF.Exp), ['small'], ['small'])
                T.op('dve', lambda e: e.tensor_tensor(out=small[:, 4:5], in0=small[:, 3:4], in1=small[:, 2:3], op=ALU.subtract), ['small'], ['small'])
                T.op('dve', lambda e: e.tensor_scalar(out=small[:, 5:6], in0=small[:, 4:5], scalar1=-lam_init, scalar2=None, op0=ALU.add), ['small'], ['small'])
                T.op('dve', lambda e: e.tensor_scalar(out=small[:, 6:7], in0=vecs[:, C_GSUB + j:C_GSUB + j + 1], scalar1=1.0 - lam_init, scalar2=None, op0=ALU.mult),
                     ['small', 'vecs'], ['small'])
                T.muted = False
                nlam = small[:, 5:6]
                gsc = small[:, 6:7]

                adanorm(A1, modv[:, 0:8])
                T.muted = 'qk' in SKIP
                qk_tiles = [(p, c4, th) for p in range(4) for c4 in range(4) for th in range(2)]
                qk_w = {}
                def qk_stage_a(n):
                    p, c4, th = qk_tiles[n]
                    if c4 == 0 and th == 0:
                        s_, sl_ = ws_get('qkv')
                        qk_w[p] = (s_, sl_[:, 0:4096].rearrange("p (k c) -> p k c", c=512))
                    s_, w = qk_w[p]
                    b = 2 + (n % 2)
                    for k in range(8):
                        T.op('pe', lambda e: e.matmul(PS(b), w[:, k, c4 * 128:(c4 + 1) * 128], hT[:, k, half(th)], start=(k == 0), stop=(k == 7)),
                             [('w', s_), ('hT', th)], [('ps', b)], signal=(k == 7))
                qk_stage_a(0)
                for n, (p, c4, th) in enumerate(qk_tiles):
                    if i == 0 and c4 == 3 and th == 1:
                        mod_piece(4 + p)
                    if n + 1 < len(qk_tiles):
                        qk_stage_a(n + 1)
                    isk = p >= 2
                    hh = (p % 2) * 4 + c4
                    b = 2 + (n % 2)
                    qi = n % 2
                    if isk:
                        T.op('dve', lambda e: e.tensor_copy(out=kst[:, qi, :], in_=PS(b)), [('ps', b)], [('kst', qi)])
                        T.dma('sp', k_out[j, hh * 128:(hh + 1) * 128, half(th)], kst[:, qi, :], [('kst', qi)], [], ('o_kst', qi))
                        T.op('act', lambda e: e.activation(out=qraw[:, qi, :], in_=kst[:, qi, :], func=AF.Identity), [('kst', qi)], [('qraw', qi)])
                    else:
                        T.op('act', lambda e: e.activation(out=qraw[:, qi, :], in_=PS(b), func=AF.Identity), [('ps', b)], [('qraw', qi)])
                    rb = 4 + qi
                    T.op('pe', lambda e: e.matmul(PS(rb), cstb[:, 0, :], qraw[:, qi, :], start=True, stop=True), ['cstb', ('qraw', qi)], [('ps', rb)])
                    T.op('dve', lambda e: e.tensor_tensor(out=t1[:, 0, :], in0=qraw[:, qi, :], in1=ropet[:, 0, half(th)], op=ALU.mult),
                         [('qraw', qi), 'ropet'], ['t1'])
                    T.op('dve', lambda e: e.tensor_tensor(out=tt[:], in0=PS(rb), in1=ropet[:, 1, half(th)], op=ALU.mult),
                         [('ps', rb), 'ropet'], ['tt'])
                    dst = KT[:, hh, 512 + th * 512: 512 + (th + 1) * 512] if isk else QT[:, hh, half(th)]
                    T.op('dve', lambda e: e.tensor_tensor(out=dst, in0=t1[:, 0, :], in1=tt[:], op=ALU.add),
                         ['t1', 'tt'], ['KTn' if isk else 'QT'])
                T.muted = 'v' in SKIP
                cnt = 0
                for p in range(2):
                    s, sl = ws_get('qkv')
                    w = sl[:, 0:4096].rearrange("p (k c) -> p k c", c=512)
                    for tb in range(8):
                        b = 2 + (cnt % 2)
                        vi = cnt % 2
                        cnt += 1
                        for k in range(8):
                            T.op('pe', lambda e: e.matmul(PS(b), hT[:, k, tb * 128:(tb + 1) * 128], w[:, k, :], start=(k == 0), stop=(k == 7)),
                                 [('w', s), ('hT', tb // 4)], [('ps', b)], signal=(k == 7))
                        T.op('dve', lambda e: e.tensor_copy(out=vst[:, vi, :], in_=PS(b)), [('ps', b)], [('vst', vi)])
                        T.op('act', lambda e: e.activation(out=V[:, 4 + tb, p * 512:(p + 1) * 512], in_=vst[:, vi, :], func=AF.Identity), [('vst', vi)], ['Vn'])
                        T.dma('sp', v_out[j, tb * 128:(tb + 1) * 128, p * 512:(p + 1) * 512], vst[:, vi, :], [('vst', vi)], [], ('o_vst', vi))
                    if i == 0:
                        mod_piece(8 + p)
                if i == 0:
                    mod_piece(10)
                    mod_piece(11)
                    mod_finish(0, 'B')
                T.muted = False
                items = [(h, qt, kb) for h in range(0 if 'core' in SKIP else 8) for qt in range(4) for kb in range(12)]
                T.op('dve', lambda e: e.memset(Qp[:], 0.0), [], [('Qp', 0), ('Qp', 1)])
                def emit_score(n):
                    h, qt, kb = items[n]
                    sbk = n % 3
                    slot = (h * 4 + qt) % 2
                    qs = slice(qt * 256, (qt + 1) * 256)
                    ks = slice(kb * 128, (kb + 1) * 128)
                    T.op('pe', lambda e: e.matmul(PS(sbk), KT[:, h, ks], Qp[:, slot, :], start=True, stop=True),
                         ['KTc', 'KTn', ('Qp', slot)], [('ps', sbk)], signal=True)
                def emit_qp(g):
                    h, qt = g // 4, g % 4
                    slot = g % 2
                    qs = slice(qt * 256, (qt + 1) * 256)
                    T.op('pool', lambda e: e.tensor_copy(out=Qp[0:64, slot, 0:256], in_=QT[0:64, h, qs]), ['QT'], [('Qp', slot)])
                    T.op('pool', lambda e: e.tensor_copy(out=Qp[64:128, slot, 256:512], in_=QT[64:128, h, qs]), ['QT'], [('Qp', slot)])
                if items:
                    emit_qp(0)
                for n0 in range(min(2, len(items))):
                    emit_score(n0)
                for n, (h, qt, kb) in enumerate(items):
                    if kb == 4 and (h * 4 + qt + 1) * 12 < len(items):
                        emit_qp(h * 4 + qt + 1)
                    if n + 2 < len(items):
                        emit_score(n + 2)
                    sbk = n % 3
                    pi = n % 3
                    qs = slice(qt * 256, (qt + 1) * 256)
                    T.op('act', lambda e: e.activation(out=Pt[:, pi, :], in_=PS(sbk), func=AF.Exp,
                                                       bias=vecs[:, C_MASK + kb * 4 + qt:C_MASK + kb * 4 + qt + 1], scale=0.125),
                         [('ps', sbk), 'vecs'], [('Pt', pi)])
                    T.op('pe', lambda e: e.matmul(PS(4), V[:, kb, h * 128:(h + 1) * 128], Pt[:, pi, :], start=(kb == 0), stop=(kb == 11)),
                         ['Vc', 'Vn', ('Pt', pi)], [('ps', 4)], signal=False)
                    T.op('pe', lambda e: e.matmul(PS(5), ones1[:], Pt[:, pi, :], start=(kb == 0), stop=(kb == 11)),
                         ['ones1', ('Pt', pi)], [('ps', 5)], signal=True)
                    if kb != 11:
                        continue
                    T.op('dve', lambda e: e.tensor_copy(out=tt[:], in_=PS(4)), [('ps', 4)], ['tt'])
                    T.op('dve', lambda e: e.tensor_copy(out=rr[:], in_=PS(5)), [('ps', 5)], ['rr'])
                    T.op('dve', lambda e: e.reciprocal(out=rr[:], in_=rr[:]), ['rr'], ['rr'])
                    T.op('dve', lambda e: e.tensor_tensor(out=tt[:], in0=tt[:], in1=rr[:], op=ALU.mult), ['tt', 'rr'], ['tt'])
                    T.op('dve', lambda e: e.scalar_tensor_tensor(out=Oh[:, 0, qs], in0=tt[:, 256:512], scalar=nlam, in1=tt[:, 0:256],
                                                                 op0=ALU.mult, op1=ALU.add), ['tt', 'small'], [('Oh', 0)])
                    if qt != 3:
                        continue
                    for th in range(2):
                        T.op('act', lambda e: e.activation(out=sqo[:, th, :], in_=Oh[:, 0, half(th)], func=AF.Square), [('Oh', 0)], [('sqo', th)])
                        T.op('pe', lambda e: e.matmul(PS(6), ones128[:], sqo[:, th, :], start=True, stop=True), ['ones128', ('sqo', th)], [('ps', 6)])
                        T.op('act', lambda e: e.activation(out=lnt[:], in_=PS(6), func=AF.Ln, bias=EPS, scale=1.0), [('ps', 6)], ['lnt'])
                        T.op('act', lambda e: e.activation(out=rstd[:], in_=lnt[:], func=AF.Exp, scale=-0.5), ['lnt'], ['rstd'])
                        T.op('dve', lambda e: e.scalar_tensor_tensor(out=hT[:, h, half(th)], in0=Oh[:, 0, half(th)], scalar=gsc, in1=rstd[:],
                                                                     op0=ALU.mult, op1=ALU.mult), [('Oh', 0), 'small', 'rstd'], [('hT', th)])
                T.muted = 'wo' in SKIP
                out_proj_residual(hT, 16, 'hT')
                T.muted = False
                T.barrier()

        def hgrn(i, j):
            with contextlib.ExitStack() as ph:
                ON = sb("ON", [128, 8, NTOK], BF16, ph)
                qh = sb("qh", [128, NTOK], F32, ph)
                gateh = sb("gateh", [128, NTOK], F32, ph)
                TH = sb("TH", [128, 2, NTOK], F32, ph)
                Fb = sb("Fb", [128, NTOK], F32, ph)
                Bc = sb("Bc", [128, NTOK], F32, ph)
                E1 = sb("E1", [128, NTOK], F32, ph)
                E2 = sb("E2", [128, NTOK], F32, ph)
                Dd = sb("Dd", [128, 2, 64], F32, ph)
                Qt = sb("Qt", [128, 2, NTOK], BF16, ph)
                Kt = sb("Kt", [128, NTOK], BF16, ph)
                Sb = sb("Sb", [128, 2, 3, 128], BF16, ph)
                Kh = sb("Kh", [128, NTOK], BF16, ph)
                Ktm = sb("Ktm", [128, 2, 8, 128], BF16, ph)
                Vh = sb("Vh", [128, 8, 128], BF16, ph)
                Vexp = sb("Vexp", [128, 8, 8, 128], BF16, ph)
                cmk = sb("cmk", [128, 8, 128], BF16, ph)
                hmk = sb("hmk", [128, 2, 128], F32, ph)
                rmask = sb("rmask", [128, 1, NTOK], F32, ph)
                Mf = sb("Mf", [128, 2, NTOK], BF16, ph)
                Mb = sb("Mb", [128, NTOK], BF16, ph)
                Sd = sb("Sd", [128, 2, 2, 128], F32, ph)
                S0 = sb("S0", [128, 2, 128], F32, ph)
                stage = sb("stage", [128, 4, 2, 128], F32, ph)
                lbt = sb("lbt", [128, 64], F32, ph)
                lbs = sb("lbs", [128, 80], F32, ph)
                sqo = sb("sqo2", [128, 512], BF16, ph)
                T.dma('pool', cmk[:], cmask_in, [], ['cmk'], 'ld_cmk')
                T.dma('sp', hmk[:], hmask_in, [], ['hmk'], 'ld_hmk')
                r4 = rmask[:].rearrange("p d (c s) -> p d c s", s=16)
                T.op('dve', lambda e: e.memset(rmask[:], 1.0), [], ['rmask'])
                T.op('dve', lambda e: e.memset(r4[:, 0, :, 0:1], 0.0), [], ['rmask'])
                T.op('act', lambda e: e.activation(out=lbt[:], in_=vecs[:, C_LBL:C_LBL + 64], func=AF.Exp), ['vecs'], ['lbt'])
                l4 = lbt[:].rearrange("p (d l h) -> p d l h", d=2, l=4)
                den = lbs[:, 0:16].rearrange("p (d h) -> p d h", d=2)
                num = lbs[:, 16:32].rearrange("p (d h) -> p d h", d=2)
                lb = lbs[:, 32:48].rearrange("p (d h) -> p d h", d=2)
                c0 = lbs[:, 48:64].rearrange("p (d h) -> p d h", d=2)
                c1 = lbs[:, 64:80].rearrange("p (d h) -> p d h", d=2)
                T.op('dve', lambda e: e.tensor_tensor(out=den, in0=l4[:, :, 0, :], in1=l4[:, :, 1, :], op=ALU.add), ['lbt'], ['lbs'])
                T.op('dve', lambda e: e.tensor_tensor(out=den, in0=den, in1=l4[:, :, 2, :], op=ALU.add), ['lbt', 'lbs'], ['lbs'])
                T.op('dve', lambda e: e.tensor_tensor(out=den, in0=den, in1=l4[:, :, 3, :], op=ALU.add), ['lbt', 'lbs'], ['lbs'])
                T.op('dve', lambda e: e.tensor_copy(out=num, in_=l4[:, :, 1, :]), ['lbt', 'lbs'], ['lbs'])
                for l in range(2, i + 1):
                    T.op('dve', lambda e: e.tensor_tensor(out=num, in0=num, in1=l4[:, :, l, :], op=ALU.add), ['lbt', 'lbs'], ['lbs'])
                T.op('dve', lambda e: e.reciprocal(out=den, in_=den), ['lbs'], ['lbs'])
                T.op('dve', lambda e: e.tensor_tensor(out=lb, in0=num, in1=den, op=ALU.mult), ['lbs'], ['lbs'])
                T.op('dve', lambda e: e.tensor_scalar(out=c0, in0=lb, scalar1=0.5, scalar2=0.5, op0=ALU.mult, op1=ALU.add), ['lbs'], ['lbs'])
                T.op('dve', lambda e: e.tensor_scalar(out=c1, in0=lb, scalar1=-0.5, scalar2=0.5, op0=ALU.mult, op1=ALU.add), ['lbs'], ['lbs'])
                nc1 = small[:, 8:10]

                adanorm(A1, modv[:, 0:8])
                for h in range(8):
                    s, sl = ws_get('rec')
                    wv = lambda g: sl[:, g * 1024:(g + 1) * 1024].rearrange("p (k c) -> p k c", c=128)
                    T.dma('sp', S0[:], s0_in[h], [], ['S0'], 'ld_s0')
                    cnt = 0
                    for (g, kind_) in ((0, 'q'), (2, 'g'), (3, 'zf'), (4, 'zb')):
                        w = wv(g)
                        for th in range(2):
                            b = 2 + (cnt % 2)
                            cnt += 1
                            for k in range(8):
                                T.op('pe', lambda e: e.matmul(PS(b), w[:, k, :], hT[:, k, half(th)], start=(k == 0), stop=(k == 7)),
                                     [('w', s), ('hT', th)], [('ps', b)], signal=(k == 7))
                            if kind_ == 'q':
                                T.op('act', lambda e: e.activation(out=qh[:, half(th)], in_=PS(b), func=AF.Silu), [('ps', b)], ['qh'])
                            elif kind_ == 'g':
                                T.op('act', lambda e: e.activation(out=gateh[:, half(th)], in_=PS(b), func=AF.Silu), [('ps', b)], ['gateh'])
                            else:
                                d = 0 if kind_ == 'zf' else 1
                                T.op('act', lambda e: e.activation(out=TH[:, d, half(th)], in_=PS(b), func=AF.Tanh, scale=0.5), [('ps', b)], [('TH', d)])
                    w = wv(1)
                    for tq in range(2):
                        b = 2 + (cnt % 2)
                        cnt += 1
                        for t4 in range(4):
                            tb = tq * 4 + t4
                            for k in range(8):
                                T.op('pe', lambda e: e.matmul(ps[:, b, t4 * 128:(t4 + 1) * 128], hT[:, k, tb * 128:(tb + 1) * 128], w[:, k, :],
                                                              start=(k == 0), stop=(k == 7)),
                                     [('w', s), ('hT', tb // 4)], [('ps', b)], signal=(k == 7 and t4 == 3))
                        T.op('act', lambda e: e.activation(out=Vh[:, tq * 4:(tq + 1) * 4, :], in_=PS(b).rearrange("p (t v) -> p t v", t=4), func=AF.Identity),
                             [('ps', b)], ['Vh'])
                    for tb in range(8):
                        T.op('pool', lambda e: e.tensor_tensor(out=Vexp[:, tb, :, :], in0=Vh[:, tb:tb + 1, :].to_broadcast([128, 8, 128]), in1=cmk[:], op=ALU.mult),
                             ['Vh', 'cmk'], ['Vexp'])
                    for d in range(2):
                        c0d, c1d = c0[:, d, h:h + 1], c1[:, d, h:h + 1]
                        T.op('dve', lambda e: e.tensor_scalar(out=nc1[:, d:d + 1], in0=c1d, scalar1=-1.0, scalar2=None, op0=ALU.mult), ['lbs'], [('nc1', d)])
                        T.op('act', lambda e: e.activation(out=Fb[:], in_=TH[:, d, :], func=AF.Identity, scale=c1d, bias=c0d),
                             [('TH', d), 'lbs'], ['Fb'])
                        T.op('act', lambda e: e.activation(out=TH[:, d, :], in_=TH[:, d, :], func=AF.Identity, scale=nc1[:, d:d + 1], bias=c1d),
                             [('TH', d), 'lbs', ('nc1', d)], [('TH', d)])
                        T.op('act', lambda e: e.activation(out=Fb[:], in_=Fb[:], func=AF.Ln), ['Fb'], ['Fb'])
                        if d == 0:
                            T.op('dve', lambda e: e.tensor_tensor_scan(out=Bc[:], data0=rmask[:, 0, :], data1=Fb[:], initial=0.0, op0=ALU.mult, op1=ALU.add),
                                 ['Fb', 'rmask'], ['Bc'])
                        else:
                            T.op('dve', lambda e: e.tensor_tensor_scan(out=Bc[:, ::-1], data0=rmask[:, 0, :], data1=Fb[:, ::-1], initial=0.0,
                                                                       op0=ALU.mult, op1=ALU.add), ['Fb', 'rmask'], ['Bc'])
                        T.op('act', lambda e: e.activation(out=E1[:], in_=Bc[:], func=AF.Exp), ['Bc'], ['E1'])
                        T.op('act', lambda e: e.activation(out=E2[:], in_=Bc[:], func=AF.Exp, scale=-1.0), ['Bc'], ['E2'])
                        e14 = E1[:].rearrange("p (c s) -> p c s", s=16)
                        lastpos = 15 if d == 0 else 0
                        T.op('dve', lambda e: e.tensor_copy(out=Dd[:, d, :], in_=e14[:, :, lastpos]), ['E1'], [('Dd', d)])
                        T.op('dve', lambda e: e.tensor_tensor(out=Qt[:, d, :], in0=qh[:], in1=E1[:], op=ALU.mult), ['qh', 'E1'], [('Qt', d)])
                        T.op('dve', lambda e: e.tensor_tensor(out=Bc[:], in0=TH[:, d, :], in1=E2[:], op=ALU.mult), [('TH', d), 'E2', 'Bc'], ['Bc'])
                        T.op('act', lambda e: e.activation(out=Kt[:], in_=Bc[:], func=AF.Identity), ['Bc'], ['Kt'])
                        T.op('dve', lambda e: e.tensor_tensor(out=Kh[:].rearrange("p (c s) -> p c s", s=16), in0=Bc[:].rearrange("p (c s) -> p c s", s=16),
                                                              in1=Dd[:, d, :].unsqueeze(2).to_broadcast([128, 64, 16]), op=ALU.mult),
                             ['Bc', ('Dd', d)], ['Kh'])
                        for tb in range(8):
                            T.op('pe', lambda e: e.transpose(pbf[:, tb * 128:(tb + 1) * 128], Kh[:, tb * 128:(tb + 1) * 128], cstb[:, 1, :]),
                                 ['Kh', 'cstb'], ['pbf'], signal=(tb == 7))
                        T.op('act', lambda e: e.activation(out=Ktm[:, d, :, :], in_=pbf[:].rearrange("p (t k) -> p t k", t=8), func=AF.Identity), ['pbf'], [('Ktm', d)])
                        for tq in range(2):
                            b = 2 + (cnt % 2)
                            cnt += 1
                            for t4 in range(4):
                                tb = tq * 4 + t4
                                bs = slice(tb * 128, (tb + 1) * 128)
                                T.op('pe', lambda e: e.matmul(ps[:, b, t4 * 128:(t4 + 1) * 128], Kt[:, bs], Qt[:, d, bs], start=True, stop=True),
                                     ['Kt', ('Qt', d)], [('ps', b)], signal=(t4 == 3))
                            T.op('dve', lambda e: e.tensor_tensor(out=Mf[:, d, tq * 512:(tq + 1) * 512].rearrange("p (t k) -> p t k", t=4),
                                                                  in0=PS(b).rearrange("p (t k) -> p t k", t=4),
                                                                  in1=hmk[:, d:d + 1, :].to_broadcast([128, 4, 128]), op=ALU.mult),
                                 [('ps', b), 'hmk'], [('Mf', d)])
                    T.op('dve', lambda e: e.tensor_tensor(out=Mb[:], in0=Mf[:, 0, :], in1=Mf[:, 1, :], op=ALU.add), [('Mf', 0), ('Mf', 1)], ['Mb'])
                    for bk in range(2):
                        T.op('pe', lambda e: e.matmul(PS(bk), zerosW[:], Mb[:, 0:512], start=True, stop=False, skip_group_check=True),
                             ['zerosW', 'Mb'], [('ps', bk)], signal=False)
                    for tb in range(8):
                        T.op('pe', lambda e: e.matmul(ps[:, tb // 4, (tb % 4) * 128:(tb % 4 + 1) * 128], Vh[:, tb, :], Mb[:, tb * 128:(tb + 1) * 128],
                                                      start=False, stop=False, skip_group_check=True),
                             ['Vh', 'Mb'], [('ps', tb // 4)], signal=(tb % 4 == 3))
                    T.op('dve', lambda e: e.tensor_copy(out=Sd[:, :, 0, :], in_=S0[:]), ['S0'], [('S', 0, 0), ('S', 1, 0)])
                    cur = [0, 0]
                    kvbank = lambda d, g: 3 + 2 * d + (g % 2)
                    def emit_kv(d, g):
                        tb, hf = g // 2, g % 2
                        b = kvbank(d, g)
                        T.op('pe', lambda e: e.matmul(PS(b), Ktm[:, d, tb, :], Vexp[:, tb, hf * 4:(hf + 1) * 4, :].rearrange("p c v -> p (c v)"),
                                                      start=True, stop=True),
                             [('Ktm', d), 'Vexp'], [('ps', b)])
                    emit_kv(0, 0)
                    emit_kv(1, 15)
                    for step in range(64):
                        for d in range(2):
                            c = step if d == 0 else 63 - step
                            g = c // 4
                            first_in_group = (c % 4 == 0) if d == 0 else (c % 4 == 3)
                            if first_in_group:
                                gn = g + 1 if d == 0 else g - 1
                                if 0 <= gn < 16:
                                    emit_kv(d, gn)
                            seq_start = (c % 16 == 0 and c > 0) if d == 0 else (c % 16 == 15 and c < 63)
                            cu, nx = cur[d], 1 - cur[d]
                            if seq_start:
                                T.op('dve', lambda e: e.tensor_scalar(out=Sd[:, d, cu, :], in0=Sd[:, d, cu, :], scalar1=vecs[:, C_KEEP:C_KEEP + 1], scalar2=None, op0=ALU.mult),
                                     [('S', d, cu), 'vecs'], [('S', d, cu)])
                            si = step % 3
                            if False:
                                T.op('pool', lambda e: e.tensor_copy(out=Sb[:, d, si, :], in_=Sd[:, d, cu, :]), [('S', d, cu)], [('Sb', d, si)])
                            else:
                                T.op('act', lambda e: e.activation(out=Sb[:, d, si, :], in_=Sd[:, d, cu, :], func=AF.Identity), [('S', d, cu)], [('Sb', d, si)])
                            T.op('pe', lambda e: e.matmul(ps[:, c // 32, (c % 32) * 16:(c % 32 + 1) * 16], Sb[:, d, si, :], Qt[:, d, c * 16:(c + 1) * 16],
                                                          start=False, stop=(step == 63), skip_group_check=True),
                                 [('Sb', d, si), ('Qt', d)], [('ps', c // 32)])
                            b = kvbank(d, g)
                            T.op('dve', lambda e: e.scalar_tensor_tensor(out=Sd[:, d, nx, :], in0=Sd[:, d, cu, :], scalar=Dd[:, d, c:c + 1],
                                                                         in1=ps[:, b, (c % 4) * 128:(c % 4 + 1) * 128], op0=ALU.mult, op1=ALU.add),
                                 [('S', d, cu), ('Dd', d), ('ps', b)], [('S', d, nx)])
                            seq_end = (c % 16 == 15) if d == 0 else (c % 16 == 0)
                            if seq_end:
                                T.op('dve', lambda e: e.tensor_copy(out=stage[:, c // 16, d, :], in_=Sd[:, d, nx, :]), [('S', d, nx)], ['stage'])
                            cur[d] = nx
                    T.dma('sp', s_out[:, :, h].rearrange("s d k v -> k s d v"), stage[:], ['stage'], [], 'o_stage')
                    for th in range(2):
                        T.op('act', lambda e: e.activation(out=sqo[:], in_=PS(th), func=AF.Square), [('ps', th)], ['sqo'])
                        T.op('pe', lambda e: e.matmul(PS(2), ones128[:], sqo[:], start=True, stop=True), ['ones128', 'sqo'], [('ps', 2)])
                        T.op('act', lambda e: e.activation(out=lnt[:], in_=PS(2), func=AF.Ln, bias=EPS, scale=1.0), [('ps', 2)], ['lnt'])
                        T.op('act', lambda e: e.activation(out=rstd[:], in_=lnt[:], func=AF.Exp, scale=-0.5), ['lnt'], ['rstd'])
                        T.op('dve', lambda e: e.scalar_tensor_tensor(out=tmpA[:, 0, :], in0=PS(th), scalar=vecs[:, C_GOUT:C_GOUT + 1], in1=rstd[:],
                                                                     op0=ALU.mult, op1=ALU.mult), [('ps', th), 'vecs', 'rstd'], [('tmpA', 0)])
                        T.op('dve', lambda e: e.tensor_tensor(out=ON[:, h, half(th)], in0=tmpA[:, 0, :], in1=gateh[:, half(th)], op=ALU.mult),
                             [('tmpA', 0), 'gateh'], [('ON', th)])
                out_proj_residual(ON, 16, 'ON')
                T.barrier()

        def fourier(i, j):
            with contextlib.ExitStack() as ph:
                CN = sb("CN", [128, 2, 8, NTOK], BF16, ph)
                CS = sb("CS", [128, 2, 512], BF16, ph)
                Af = sb("Af", [128, 8, 4, 512], BF16, ph)
                for a in range(2):
                    T.dma('pool', CN[:, a, :, :], dftn_in[a].rearrange("(kb p) n -> p kb n", p=128), [], ['CN'], 'ld_cn')
                T.dma('pool', CS[:], dftc_in.rearrange("(cc p) n -> p cc n", p=128), [], ['CS'], 'ld_cs')
                adanorm(A1, modv[:, 0:8])
                cnt = 0
                for tb in range(8):
                    for g in range(4):
                        b = 2 + (cnt % 2)
                        cnt += 1
                        for cc in range(2):
                            T.op('pe', lambda e: e.matmul(PS(b), hT[:, g * 2 + cc, tb * 128:(tb + 1) * 128], CS[:, cc, :], start=(cc == 0), stop=(cc == 1)),
                                 [('hT', tb // 4), 'CS'], [('ps', b)], signal=(cc == 1))
                        if cnt % 2:
                            T.op('act', lambda e: e.activation(out=Af[:, tb, g, :], in_=PS(b), func=AF.Identity), [('ps', b)], ['Af'])
                        else:
                            T.op('dve', lambda e: e.tensor_copy(out=Af[:, tb, g, :], in_=PS(b)), [('ps', b)], ['Af'])
                for fc in range(8):
                    g, cq = fc // 2, fc % 2
                    for th in range(2):
                        b = 2 + (cnt % 2)
                        cnt += 1
                        n = 0
                        for tb in range(8):
                            for a in range(2):
                                T.op('pe', lambda e: e.matmul(PS(b), Af[:, tb, g, a * 256 + cq * 128: a * 256 + (cq + 1) * 128], CN[:, a, tb, half(th)],
                                                              start=(n == 0), stop=(n == 15)),
                                     ['Af', 'CN'], [('ps', b)], signal=(n == 15))
                                n += 1
                        T.op('act', lambda e: e.activation(out=hT[:, fc, half(th)], in_=PS(b), func=AF.Identity), [('ps', b)], [('hT', th)])
                out_proj_residual(hT, 16, 'hT')
                T.barrier()

        for p in range(4):
            mod_piece(p)
        for i in range(depth_run):
            kind, j = i % 3, i // 3
            mod_finish(i, 'A' if i == 0 else 'all')
            if 'mixer' in SKIP:
                for _ in range(16 if kind == 1 else (8 if kind == 0 else 2)):
                    ws['next'] += 1
                ffn(i)
                continue
            if kind == 0:
                attention(i, j)
            elif kind == 1:
                hgrn(i, j)
            else:
                fourier(i, j)
            ffn(i)
        if True:
            gfin = vecs[:, C_GFIN:C_GFIN + 8]
            for th in range(2):
                def out_fn(k, ti, th=th):
                    T.dma('sp', y_out[k * 128:(k + 1) * 128, half(th)], tmpA[:, ti, :], [('tmpA', ti)], [], ('o_y', ti))
                norm_apply(th, gfin, None, out_fn)
        for key in list(T.sems.keys()):
            if isinstance(key, tuple) and isinstance(key[0], str) and key[0].startswith('o_') or key == 'o_stage':
                nc.sync.wait_ge(T.sems[key], T.count[key])
        T.barrier()
    return nc


def _host_tables(is_sample):
    p = np.arange(128)
    d = p % 64
    t = np.arange(NTOK)
    if is_sample:
        row, col = t // 64, t % 64
        dd = d % 32
        idx = dd % 16
        inv = 10000.0 ** (-(idx.astype(np.float64)) / 16.0)
        pos = np.where((d < 32)[:, None], row[None, :], col[None, :]).astype(np.float64)
        ang = pos * inv[:, None]
        cos = np.cos(ang)
        sin = np.sin(ang)
        sin = np.where((dd < 16)[:, None], -sin, sin)
    else:
        cos = np.ones((128, NTOK))
        sin = np.zeros((128, NTOK))
    rope = np.stack([cos, sin]).astype(np.float32)
    if is_sample:
        n = np.arange(NTOK, dtype=np.float64)
        ang = 2 * np.pi * np.outer(n, n) / NTOK
        norm = 1.0 / math.sqrt(NTOK * 256.0)
        cn, sn = np.cos(ang) * norm, -np.sin(ang) * norm
    else:
        n = np.arange(256, dtype=np.float64)
        ang = 2 * np.pi * np.outer(n, n) / 256.0
        norm = 1.0 / 256.0
        cn = np.zeros((NTOK, NTOK)); sn = np.zeros((NTOK, NTOK))
        for s in range(4):
            cn[s * 256:(s + 1) * 256, s * 256:(s + 1) * 256] = np.cos(ang) * norm
            sn[s * 256:(s + 1) * 256, s * 256:(s + 1) * 256] = -np.sin(ang) * norm
    dftn = np.stack([cn, sn]).astype(np.float32)
    c = np.arange(256, dtype=np.float64)
    angc = 2 * np.pi * np.outer(c, c) / 256.0
    dftc = np.concatenate([np.cos(angc), np.sin(angc)], axis=1).astype(np.float32)
    return rope, dftn, dftc


def _const_tables():
    R = np.zeros((128, 128), np.float32)
    for p in range(128):
        partner = p + 16 if (p % 32) < 16 else p - 16
        R[partner, p] = 1.0
    cst = np.stack([R, np.eye(128, dtype=np.float32)])
    s = np.arange(128)
    same = (s[:, None] // 16) == (s[None, :] // 16)
    hmask = np.stack([same & (s[:, None] <= s[None, :]), same & (s[:, None] >= s[None, :])], axis=1).astype(np.float32)
    cmask = np.zeros((128, 8, 128), np.float32)
    for cc in range(8):
        cmask[cc * 16:(cc + 1) * 16, cc, :] = 1.0
    return cst, hmask, cmask


def _pack_vecs(inp, cond, is_sample):
    v = np.zeros((128, NV), np.float32)
    fm = lambda a: np.asarray(a, np.float32).reshape(-1, 128).T
    v[:, C_COND:C_COND + 8] = fm(cond)
    for i in range(DEPTH):
        v[:, C_GMIX + i * 8:C_GMIX + (i + 1) * 8] = fm(inp['g_norm_mix'][i])
        v[:, C_GFFN + i * 8:C_GFFN + (i + 1) * 8] = fm(inp['g_norm_ffn'][i])
        v[:, C_BADA + i * 48:C_BADA + (i + 1) * 48] = fm(inp['b_ada'][i])
    v[:, C_GFIN:C_GFIN + 8] = fm(inp['g_final'])
    for j in range(2):
        v[:, C_GSUB + j] = inp['g_subln_attn'][j]
    v[:, C_GOUT] = inp['g_out_rec'][0]
    lbl = np.asarray(inp['lb_logits_rec'], np.float32).reshape(2, 4, 8, 128)
    v[:, C_LBL:C_LBL + 64] = lbl.transpose(3, 0, 1, 2).reshape(128, 64)
    if not is_sample:
        m = np.full((12, 4), -30000.0, np.float32)
        for kb in range(4, 12):
            m[kb, (kb - 4) // 2] = 0.0
        v[:, C_MASK:C_MASK + 48] = m.reshape(1, 48)
    v[:, C_KEEP] = 1.0 if is_sample else 0.0
    v[:, C_LAM:C_LAM + 512] = np.asarray(inp['lam_attn'], np.float32).reshape(1, 512)
    return v


_CACHE = {}


def kernel(**inp):
    inp = {k: np.asarray(v) for k, v in inp.items()}
    if 'nc' not in _CACHE:
        _CACHE['nc'] = build_program(int(os.environ.get('KDEPTH', DEPTH)))
    nc = _CACHE['nc']
    cst, hmask, cmask = _const_tables()
    tabs = {True: _host_tables(True), False: _host_tables(False)}
    shared = {
        'w_ada': inp['w_ada'], 'w_qkv': inp['w_qkv_attn'], 'w_oa': inp['w_o_attn'], 'w_rin': inp['w_in_rec'],
        'w_or': inp['w_o_rec'], 'w_four': inp['w_four'], 'w_fin': inp['w_ffn_in'], 'w_fout': inp['w_ffn_out'],
        'cst': cst, 'hmask': hmask, 'cmask': cmask,
    }
    shared = {k: np.ascontiguousarray(v, dtype=np.float32) for k, v in shared.items()}
    in_maps = []
    for r in range(8):
        is_sample = r < 4
        rope, dftn, dftc = tabs[is_sample]
        m = dict(shared)
        if is_sample:
            b = r
            x = inp['x_sample'][b]
            m['kcT'] = np.ascontiguousarray(inp['cache_attn_k'][b].reshape(2, 512, D).transpose(0, 2, 1))
            m['vc'] = np.ascontiguousarray(inp['cache_attn_v'][b].reshape(2, 512, D))
            m['s0'] = np.ascontiguousarray(inp['state_hgrn'][b, 0].transpose(1, 2, 0, 3))
            cond = inp['c'][b]
        else:
            q = r - 4
            x = inp['x_prompt'][4 * q:4 * q + 4].reshape(NTOK, D)
            m['kcT'] = np.zeros((2, D, 512), np.float32)
            m['vc'] = np.zeros((2, 512, D), np.float32)
            m['s0'] = np.zeros((8, 128, 2, 128), np.float32)
            cond = inp['c_ctx']
        m['xT'] = np.ascontiguousarray(x.T.astype(np.float32))
        m['vecs'] = _pack_vecs(inp, cond, is_sample)
        m['rope'], m['dftn'], m['dftc'] = rope, dftn, dftc
        in_maps.append(m)
    res = run_bass_kernel_spmd(nc, in_maps[:NCORES], core_ids=list(range(NCORES)))
    outs = res.results
    if NCORES < 8:
        return outs
    y_sample = np.stack([outs[r]['yT'].T for r in range(4)]).astype(np.float32)
    y_prompt = np.concatenate([outs[r]['yT'].T.reshape(4, 256, D) for r in range(4, 8)]).astype(np.float32)
    nk = np.concatenate([outs[r]['koutT'].transpose(2, 0, 1).reshape(4, 256, 2, D).transpose(0, 2, 1, 3) for r in range(4, 8)])
    new_k = np.ascontiguousarray(nk.reshape(16, 2, 256, 8, 2, 64)).astype(np.float32)
    nv = np.concatenate([outs[r]['vout'].reshape(2, 4, 256, D).transpose(1, 0, 2, 3) for r in range(4, 8)])
    new_v = np.ascontiguousarray(nv.reshape(16, 2, 256, 8, 128)).astype(np.float32)
    ns = np.concatenate([outs[r]['sout'] for r in range(4, 8)])
    new_s = np.ascontiguousarray(ns.reshape(16, 1, 2, 8, 128, 128)).astype(np.float32)
    return (np.ascontiguousarray(y_prompt), np.ascontiguousarray(y_sample), new_k, new_v, new_s)
```

```python
import contextlib
import math
import os
import numpy as np
import concourse.bass as bass
import concourse.mybir as mybir
from concourse.bass_utils import run_bass_kernel_spmd

F32 = mybir.dt.float32
BF16 = mybir.dt.bfloat16
AF = mybir.ActivationFunctionType
ALU = mybir.AluOpType
AX = mybir.AxisListType

D = 1024
DEPTH = 4
DFF = 2816
NTOK = 1024
EPS = 1e-6
NB = 3
SKIP = set(os.environ.get('KSKIP', '').split(','))
NCORES = int(os.environ.get('KCORES', 8))
SLOT = 5120

C_COND = 0
C_GMIX = C_COND + 8
C_GFFN = C_GMIX + 32
C_BADA = C_GFFN + 32
C_GFIN = C_BADA + 192
C_GSUB = C_GFIN + 8
C_GOUT = C_GSUB + 2
C_LBL = C_GOUT + 1
C_MASK = C_LBL + 64
C_KEEP = C_MASK + 48
C_LAM = C_KEEP + 1
NV = C_LAM + 512


class Tracker:
    def __init__(self, nc, es):
        self.nc = nc
        self.es = es
        self.eng = {'pe': nc.tensor, 'act': nc.scalar, 'dve': nc.vector, 'pool': nc.gpsimd, 'sp': nc.sync}
        self.sems = {}
        self.count = {}
        self.waited = {}
        self.lastw = {}
        self.readers = {}
        self.muted = False

    def sem(self, key):
        if key not in self.sems:
            name = "s_" + "_".join(str(k) for k in (key if isinstance(key, tuple) else (key,)))
            self.sems[key] = self.es.enter_context(self.nc.semaphore(name))
            self.count[key] = 0
        return self.sems[key]

    def _deps(self, reads, writes):
        deps = {}
        def add(sk, v):
            if v > deps.get(sk, 0):
                deps[sk] = v
        for r in reads:
            if r in self.lastw:
                add(*self.lastw[r])
        for w in writes:
            if w in self.lastw:
                add(*self.lastw[w])
            for sk, v in self.readers.get(w, {}).items():
                add(sk, v)
        return deps

    def _wait(self, e, deps):
        for sk, v in deps.items():
            if sk == 'pe' and e == 'pe':
                continue
            if self.waited.get((e, sk), 0) >= v:
                continue
            self.eng[e].wait_ge(self.sems[sk], v)
            self.waited[(e, sk)] = v

    def _record(self, sk, tick, reads, writes):
        for w in writes:
            self.lastw[w] = (sk, tick)
            self.readers[w] = {}
        for r in reads:
            d = self.readers.setdefault(r, {})
            if tick > d.get(sk, 0):
                d[sk] = tick

    def op(self, e, fn, reads=(), writes=(), signal=True):
        if self.muted:
            return None
        self.sem(e)
        self._wait(e, self._deps(reads, writes))
        inst = fn(self.eng[e])
        if signal:
            self.count[e] += 1
            inst.then_inc(self.sems[e], 1)
            tick = self.count[e]
        else:
            tick = self.count[e] + 1
        self._record(e, tick, reads, writes)
        return inst

    def dma(self, q, out, in_, reads, writes, semkey):
        if self.muted:
            return None
        self.sem(semkey)
        self._wait(q, self._deps(reads, writes))
        self.eng[q].dma_start(out=out, in_=in_).then_inc(self.sems[semkey], 16)
        self.count[semkey] += 16
        self._record(semkey, self.count[semkey], reads, writes)

    def barrier(self):
        for e in ('pe', 'act', 'dve', 'pool', 'sp'):
            for sk in ('pe', 'act', 'dve', 'pool'):
                if sk in self.sems and sk != e:
                    v = self.count[sk]
                    if v > self.waited.get((e, sk), 0):
                        self.eng[e].wait_ge(self.sems[sk], v)
                        self.waited[(e, sk)] = v
            for sk in list(self.sems.keys()):
                if sk in ('pe', 'act', 'dve', 'pool'):
                    continue
                v = self.count[sk]
                if v > self.waited.get((e, sk), 0):
                    self.eng[e].wait_ge(self.sems[sk], v)
                    self.waited[(e, sk)] = v


def build_program(depth_run=DEPTH):
    nc = bass.Bass("TRN2", target_bir_lowering=False)
    dram = lambda name, shape, kind: nc.dram_tensor(name, shape, F32, kind=kind).ap()
    IN, OUT = "ExternalInput", "ExternalOutput"
    x_in = dram("xT", [D, NTOK], IN)
    kc_in = dram("kcT", [2, D, 512], IN)
    vc_in = dram("vc", [2, 512, D], IN)
    s0_in = dram("s0", [8, 128, 2, 128], IN)
    vecs_in = dram("vecs", [128, NV], IN)
    rope_in = dram("rope", [2, 128, NTOK], IN)
    cst_in = dram("cst", [2, 128, 128], IN)
    hmask_in = dram("hmask", [128, 2, 128], IN)
    cmask_in = dram("cmask", [128, 8, 128], IN)
    dftn_in = dram("dftn", [2, NTOK, NTOK], IN)
    dftc_in = dram("dftc", [256, 512], IN)
    w_ada = dram("w_ada", [DEPTH, D, 6 * D], IN)
    w_qkv = dram("w_qkv", [2, D, 3 * D], IN)
    w_oa = dram("w_oa", [2, D, D], IN)
    w_rin = dram("w_rin", [1, D, 5 * D], IN)
    w_or = dram("w_or", [1, D, D], IN)
    w_four = dram("w_four", [1, D, D], IN)
    w_fin = dram("w_fin", [DEPTH, D, 2 * DFF], IN)
    w_fout = dram("w_fout", [DEPTH, DFF, D], IN)
    y_out = dram("yT", [D, NTOK], OUT)
    k_out = dram("koutT", [2, D, NTOK], OUT)
    v_out = dram("vout", [2, NTOK, D], OUT)
    s_out = dram("sout", [4, 2, 8, 128, 128], OUT)

    kp = lambda w: w.rearrange("(kc p) n -> p kc n", p=128)

    pieces = []
    def v3(n_k, n_c, off=0):
        return lambda sl: sl[:, off:off + n_k * n_c].rearrange("p (k c) -> p k c", c=n_c)
    ada_piece = lambda i, p: ('ada', [(v3(8, 512), kp(w_ada[i])[:, :, p * 512:(p + 1) * 512])])
    for p in range(4):
        pieces.append(ada_piece(0, p))
    for i in range(depth_run):
        kind, j = i % 3, i // 3
        if kind == 0:
            qkv_piece = lambda p: ('qkv', [(v3(8, 512), kp(w_qkv[j])[:, :, p * 512:(p + 1) * 512])])
            if i == 0:
                order = ['q0', 'a4', 'q1', 'a5', 'q2', 'a6', 'q3', 'a7', 'q4', 'a8', 'q5', 'a9', 'a10', 'a11']
            else:
                order = ['q0', 'q1', 'q2', 'q3', 'q4', 'q5']
            for o in order:
                pieces.append(qkv_piece(int(o[1:])) if o[0] == 'q' else ada_piece(0, int(o[1:])))
            wo = w_oa[j]
        elif kind == 1:
            for h in range(8):
                pieces.append(('rec', [(v3(8, 128, g * 1024), kp(w_rin[j])[:, :, g * 1024 + h * 128: g * 1024 + (h + 1) * 128]) for g in range(5)]))
            wo = w_or[j]
        else:
            wo = w_four[j]
        for p in range(2):
            pieces.append(('wo', [(v3(8, 512), kp(wo)[:, :, p * 512:(p + 1) * 512])]))
        for p in range(11):
            pieces.append(('ffin', [(v3(8, 256, u * 2048), kp(w_fin[i])[:, :, u * DFF + p * 256: u * DFF + (p + 1) * 256]) for u in range(2)]))
            if i + 1 < depth_run and p < 7:
                pieces.append(ada_piece(i + 1, p))
        for c in range(8):
            pieces.append(('ffout', [(v3(22, 128), kp(w_fout[i])[:, :, c * 128:(c + 1) * 128])]))
            if i + 1 < depth_run and c < 5:
                pieces.append(ada_piece(i + 1, 7 + c))

    with contextlib.ExitStack() as es:
        T = Tracker(nc, es)
        _uid = [0]
        def sb(name, shape, dt, st=es):
            _uid[0] += 1
            return st.enter_context(nc.sbuf_tensor(f"{name}_{_uid[0]}", shape, dt))
        xT = sb("xTs", [128, 8, NTOK], F32)
        hT = sb("hT", [128, 8, NTOK], BF16)
        wsl = sb("wsl", [128, NB, SLOT], BF16)
        vecs = sb("vecs_s", [128, NV], F32)
        modv = sb("modv", [128, 48], F32)
        A1 = sb("A1", [128, 8], F32)
        A2 = sb("A2", [128, 8], F32)
        scT = sb("scT", [128, 8], BF16)
        onesM = sb("onesM", [128, 128], BF16)
        ones128 = sb("ones128", [128, 128], BF16)
        ones1 = sb("ones1", [128, 128], BF16)
        zerosW = sb("zerosW", [128, 128], BF16)
        cstb = sb("cstb", [128, 2, 128], BF16)
        sq = sb("sq", [128, 8, 512], BF16)
        lnt = sb("lnt", [128, 512], F32)
        rstd = sb("rstd", [128, 512], F32)
        tmpA = sb("tmpA", [128, 2, 512], F32)
        small = sb("small", [128, 16], F32)
        ps = es.enter_context(nc.psum_tensor("ps", [128, 7, 512], F32))
        pbf = es.enter_context(nc.psum_tensor("pbf", [128, 1024], BF16))
        PS = lambda b: ps[:, b, :]

        ws = {'issued': 0, 'next': 0}
        def ws_issue_upto(n):
            while ws['issued'] <= min(n, len(pieces) - 1):
                m = ws['issued']
                s = m % NB
                for (dst_fn, src) in pieces[m][1]:
                    T.dma('pool', dst_fn(wsl[:, s, :]), src, reads=[], writes=[('w', s)], semkey=('w', s))
                ws['issued'] += 1
        def ws_get(kind):
            n = ws['next']
            assert pieces[n][0] == kind, (pieces[n][0], kind)
            ws_issue_upto(n + NB - 1)
            ws['next'] += 1
            s = n % NB
            return s, wsl[:, s, :]

        half = lambda th: slice(th * 512, (th + 1) * 512)

        T.dma('sp', vecs[:], vecs_in, [], ['vecs'], 'ld_vecs')
        for k in range(8):
            T.dma('sp', xT[:, k, :], x_in[k * 128:(k + 1) * 128, :], [], [('xT', 0), ('xT', 1)], 'ld_x')
        T.dma('pool', cstb[:], cst_in.rearrange("a p n -> p a n"), [], ['cstb'], 'ld_cst')
        T.op('dve', lambda e: e.memset(onesM[:], 1.0 / 1024.0), [], ['onesM'])
        T.op('dve', lambda e: e.memset(ones128[:], 1.0 / 128.0), [], ['ones128'])
        T.op('dve', lambda e: e.memset(ones1[:], 1.0), [], ['ones1'])
        T.op('dve', lambda e: e.memset(zerosW[:], 0.0), [], ['zerosW'])
        T.op('act', lambda e: e.activation(out=scT[:], in_=vecs[:, C_COND:C_COND + 8], func=AF.Silu), ['vecs'], ['scT'])
        ws_issue_upto(NB - 1)

        tmp_i = [0]
        def norm_apply(th, Acol, Bcol, out_fn):
            T.op('act', lambda e: e.activation(out=sq[:], in_=xT[:, :, half(th)], func=AF.Square), [('xT', th)], ['sq'])
            for k in range(8):
                T.op('pe', lambda e: e.matmul(PS(0), onesM[:], sq[:, k, :], start=(k == 0), stop=(k == 7)),
                     ['onesM', 'sq'], [('ps', 0)], signal=(k == 7))
            T.op('act', lambda e: e.activation(out=lnt[:], in_=PS(0), func=AF.Ln, bias=EPS, scale=1.0), [('ps', 0)], ['lnt'])
            T.op('act', lambda e: e.activation(out=rstd[:], in_=lnt[:], func=AF.Exp, scale=-0.5), ['lnt'], ['rstd'])
            for k in range(8):
                ti = tmp_i[0] % 2
                tmp_i[0] += 1
                T.op('dve', lambda e: e.scalar_tensor_tensor(out=tmpA[:, ti, :], in0=xT[:, k, half(th)], scalar=Acol[:, k:k + 1],
                                                             in1=rstd[:], op0=ALU.mult, op1=ALU.mult),
                     [('xT', th), 'rstd', 'mod'], [('tmpA', ti)])
                out_fn(k, ti)

        def adanorm(Acol, Bcol):
            for th in range(2):
                def out_fn(k, ti, th=th):
                    T.op('act', lambda e: e.activation(out=hT[:, k, half(th)], in_=tmpA[:, ti, :], func=AF.Identity,
                                                       bias=Bcol[:, k:k + 1], scale=1.0),
                         [('tmpA', ti), 'mod'], [('hT', th)])
                norm_apply(th, Acol, Bcol, out_fn)

        def mod_piece(p):
            s, sl = ws_get('ada')
            w = sl[:, 0:4096].rearrange("p (k c) -> p k c", c=512)
            for c4 in range(4):
                col = p * 4 + c4
                for k in range(8):
                    T.op('pe', lambda e: e.matmul(ps[:, 1, col:col + 1], w[:, k, c4 * 128:(c4 + 1) * 128], scT[:, k:k + 1],
                                                  start=(k == 0), stop=(k == 7)),
                         [('w', s), 'scT'], [('ps', 1)], signal=(k == 7 and c4 == 3))

        def mod_finish(i, part='all'):
            lo, hi = {'all': (0, 48), 'A': (0, 16), 'B': (16, 48)}[part]
            T.op('dve', lambda e: e.tensor_tensor(out=modv[:, lo:hi], in0=ps[:, 1, lo:hi], in1=vecs[:, C_BADA + i * 48 + lo:C_BADA + i * 48 + hi], op=ALU.add),
                 [('ps', 1), 'vecs'], ['mod'])
            if part in ('all', 'A'):
                T.op('dve', lambda e: e.scalar_tensor_tensor(out=A1[:], in0=modv[:, 8:16], scalar=1.0, in1=vecs[:, C_GMIX + i * 8:C_GMIX + (i + 1) * 8],
                                                             op0=ALU.add, op1=ALU.mult), ['mod', 'vecs'], ['mod'])
            if part in ('all', 'B'):
                T.op('dve', lambda e: e.scalar_tensor_tensor(out=A2[:], in0=modv[:, 32:40], scalar=1.0, in1=vecs[:, C_GFFN + i * 8:C_GFFN + (i + 1) * 8],
                                                             op0=ALU.add, op1=ALU.mult), ['mod', 'vecs'], ['mod'])

        def out_proj_residual(src, gate_off, src_key):
            bank = 0
            for p in range(2):
                s, sl = ws_get('wo')
                w = sl[:, 0:4096].rearrange("p (k c) -> p k c", c=512)
                for c4 in range(4):
                    fc = p * 4 + c4
                    for th in range(2):
                        b = 2 + (bank % 2)
                        bank += 1
                        for k in range(8):
                            T.op('pe', lambda e: e.matmul(PS(b), w[:, k, c4 * 128:(c4 + 1) * 128], src[:, k, half(th)], start=(k == 0), stop=(k == 7)),
                                 [('w', s), (src_key, th)], [('ps', b)], signal=(k == 7))
                        T.op('dve', lambda e: e.scalar_tensor_tensor(out=xT[:, fc, half(th)], in0=PS(b), scalar=modv[:, gate_off + fc:gate_off + fc + 1],
                                                                     in1=xT[:, fc, half(th)], op0=ALU.mult, op1=ALU.add),
                             [('ps', b), 'mod', ('xT', th)], [('xT', th)])

        def ffn(i):
            with contextlib.ExitStack() as ph:
                HT = sb("HT", [128, 22, NTOK], BF16, ph)
                sg = sb("sg", [128, 2, 512], F32, ph)
                adanorm(A2, modv[:, 24:32])
                cnt = 0
                for p in range(11):
                    s, sl = ws_get('ffin')
                    wg = sl[:, 0:2048].rearrange("p (k c) -> p k c", c=256)
                    wu = sl[:, 2048:4096].rearrange("p (k c) -> p k c", c=256)
                    for cc in range(2):
                        ff = p * 2 + cc
                        for th in range(2):
                            bg, bu = 2 + 2 * (cnt % 2), 3 + 2 * (cnt % 2)
                            si = cnt % 2
                            cnt += 1
                            for k in range(8):
                                T.op('pe', lambda e: e.matmul(PS(bg), wg[:, k, cc * 128:(cc + 1) * 128], hT[:, k, half(th)], start=(k == 0), stop=(k == 7)),
                                     [('w', s), ('hT', th)], [('ps', bg)], signal=(k == 7))
                            for k in range(8):
                                T.op('pe', lambda e: e.matmul(PS(bu), wu[:, k, cc * 128:(cc + 1) * 128], hT[:, k, half(th)], start=(k == 0), stop=(k == 7)),
                                     [('w', s), ('hT', th)], [('ps', bu)], signal=(k == 7))
                            T.op('act', lambda e: e.activation(out=sg[:, si, :], in_=PS(bg), func=AF.Silu), [('ps', bg)], [('sg', si)])
                            T.op('dve', lambda e: e.tensor_tensor(out=HT[:, ff, half(th)], in0=PS(bu), in1=sg[:, si, :], op=ALU.mult),
                                 [('ps', bu), ('sg', si)], [('HT', th)])
                    if i + 1 < depth_run and p < 7:
                        mod_piece(p)
                bank = 0
                for fc in range(8):
                    s, sl = ws_get('ffout')
                    w = sl[:, 0:2816].rearrange("p (k c) -> p k c", c=128)
                    for th in range(2):
                        b = 2 + (bank % 2)
                        bank += 1
                        for k in range(22):
                            T.op('pe', lambda e: e.matmul(PS(b), w[:, k, :], HT[:, k, half(th)], start=(k == 0), stop=(k == 21)),
                                 [('w', s), ('HT', th)], [('ps', b)], signal=(k == 21))
                        T.op('dve', lambda e: e.scalar_tensor_tensor(out=xT[:, fc, half(th)], in0=PS(b), scalar=modv[:, 40 + fc:41 + fc],
                                                                     in1=xT[:, fc, half(th)], op0=ALU.mult, op1=ALU.add),
                             [('ps', b), 'mod', ('xT', th)], [('xT', th)])
                    if i + 1 < depth_run and fc < 5:
                        mod_piece(7 + fc)
                T.barrier()

        def attention(i, j):
            lam_init = 0.8 - 0.6 * math.exp(-0.3 * i)
            with contextlib.ExitStack() as ph:
                QT = sb("QT", [128, 8, NTOK], BF16, ph)
                KT = sb("KT", [128, 8, 1536], BF16, ph)
                V = sb("Vtm", [128, 12, D], BF16, ph)
                Pt = sb("Pt", [128, 3, 512], BF16, ph)
                Qp = sb("Qp", [128, 2, 512], BF16, ph)
                qraw = sb("qraw", [128, 2, 512], BF16, ph)
                ropet = sb("ropet", [128, 2, NTOK], F32, ph)
                kst = sb("kst", [128, 2, 512], F32, ph)
                vst = sb("vst", [128, 2, 512], F32, ph)
                t1 = sb("t1", [128, 1, 512], F32, ph)
                rr = sb("rr", [128, 512], F32, ph)
                tt = sb("tt", [128, 512], F32, ph)
                Oh = sb("Oh", [128, 1, NTOK], F32, ph)
                sqo = sb("sqo", [128, 2, 512], BF16, ph)
                lamt = sb("lamt", [128, 2, 64], F32, ph)
                T.muted = 'aload' in SKIP
                T.dma('sp', ropet[:], rope_in.rearrange("a p n -> p a n"), [], ['ropet'], 'ld_rope')
                T.dma('pool', KT[:, :, 0:512], kc_in[j].rearrange("(h p) n -> p h n", p=128), [], ['KTc'], 'ld_kc')
                T.dma('pool', V[:, 0:4, :], vc_in[j].rearrange("(kb p) f -> p kb f", p=128), [], ['Vc'], 'ld_vc')
                T.muted = 'lam' in SKIP
                lv = vecs[:, C_LAM + j * 256:C_LAM + (j + 1) * 256].rearrange("p (a b d) -> p a b d", a=2, b=2)
                T.op('dve', lambda e: e.tensor_tensor(out=lamt[:], in0=lv[:, :, 0, :], in1=lv[:, :, 1, :], op=ALU.mult), ['vecs'], ['lamt'])
                T.op('dve', lambda e: e.tensor_reduce(out=small[:, 0:2], in_=lamt[:], axis=AX.X, op=ALU.add), ['lamt'], ['small'])
                T.op('act', lambda e: e.activation(out=small[:, 2:4], in_=small[:, 0:2], func=AF.Exp), ['small'], ['small'])
                T.op('dve', lambda e: e.tensor_tensor(out=small[:, 4:5], in0=small[:, 3:4], in1=small[:, 2:3], op=ALU.subtract), ['small'], ['small'])
                T.op('dve', lambda e: e.tensor_scalar(out=small[:, 5:6], in0=small[:, 4:5], scalar1=-lam_init, scalar2=None, op0=ALU.add), ['small'], ['small'])
                T.op('dve', lambda e: e.tensor_scalar(out=small[:, 6:7], in0=vecs[:, C_GSUB + j:C_GSUB + j + 1], scalar1=1.0 - lam_init, scalar2=None, op0=ALU.mult),
                     ['small', 'vecs'], ['small'])
                T.muted = False
                nlam = small[:, 5:6]
                gsc = small[:, 6:7]

                adanorm(A1, modv[:, 0:8])
                T.muted = 'qk' in SKIP
                qk_tiles = [(p, c4, th) for p in range(4) for c4 in range(4) for th in range(2)]
                qk_w = {}
                def qk_stage_a(n):
                    p, c4, th = qk_tiles[n]
                    if c4 == 0 and th == 0:
                        s_, sl_ = ws_get('qkv')
                        qk_w[p] = (s_, sl_[:, 0:4096].rearrange("p (k c) -> p k c", c=512))
                    s_, w = qk_w[p]
                    b = 2 + (n % 2)
                    for k in range(8):
                        T.op('pe', lambda e: e.matmul(PS(b), w[:, k, c4 * 128:(c4 + 1) * 128], hT[:, k, half(th)], start=(k == 0), stop=(k == 7)),
                             [('w', s_), ('hT', th)], [('ps', b)], signal=(k == 7))
                qk_stage_a(0)
                for n, (p, c4, th) in enumerate(qk_tiles):
                    if i == 0 and c4 == 3 and th == 1:
                        mod_piece(4 + p)
                    if n + 1 < len(qk_tiles):
                        qk_stage_a(n + 1)
                    isk = p >= 2
                    hh = (p % 2) * 4 + c4
                    b = 2 + (n % 2)
                    qi = n % 2
                    if isk:
                        T.op('dve', lambda e: e.tensor_copy(out=kst[:, qi, :], in_=PS(b)), [('ps', b)], [('kst', qi)])
                        T.dma('sp', k_out[j, hh * 128:(hh + 1) * 128, half(th)], kst[:, qi, :], [('kst', qi)], [], ('o_kst', qi))
                        T.op('act', lambda e: e.activation(out=qraw[:, qi, :], in_=kst[:, qi, :], func=AF.Identity), [('kst', qi)], [('qraw', qi)])
                    else:
                        T.op('act', lambda e: e.activation(out=qraw[:, qi, :], in_=PS(b), func=AF.Identity), [('ps', b)], [('qraw', qi)])
                    rb = 4 + qi
                    T.op('pe', lambda e: e.matmul(PS(rb), cstb[:, 0, :], qraw[:, qi, :], start=True, stop=True), ['cstb', ('qraw', qi)], [('ps', rb)])
                    T.op('dve', lambda e: e.tensor_tensor(out=t1[:, 0, :], in0=qraw[:, qi, :], in1=ropet[:, 0, half(th)], op=ALU.mult),
                         [('qraw', qi), 'ropet'], ['t1'])
                    T.op('dve', lambda e: e.tensor_tensor(out=tt[:], in0=PS(rb), in1=ropet[:, 1, half(th)], op=ALU.mult),
                         [('ps', rb), 'ropet'], ['tt'])
                    dst = KT[:, hh, 512 + th * 512: 512 + (th + 1) * 512] if isk else QT[:, hh, half(th)]
                    T.op('dve', lambda e: e.tensor_tensor(out=dst, in0=t1[:, 0, :], in1=tt[:], op=ALU.add),
                         ['t1', 'tt'], ['KTn' if isk else 'QT'])
                T.muted = 'v' in SKIP
                cnt = 0
                for p in range(2):
                    s, sl = ws_get('qkv')
                    w = sl[:, 0:4096].rearrange("p (k c) -> p k c", c=512)
                    for tb in range(8):
                        b = 2 + (cnt % 2)
                        vi = cnt % 2
                        cnt += 1
                        for k in range(8):
                            T.op('pe', lambda e: e.matmul(PS(b), hT[:, k, tb * 128:(tb + 1) * 128], w[:, k, :], start=(k == 0), stop=(k == 7)),
                                 [('w', s), ('hT', tb // 4)], [('ps', b)], signal=(k == 7))
                        T.op('dve', lambda e: e.tensor_copy(out=vst[:, vi, :], in_=PS(b)), [('ps', b)], [('vst', vi)])
                        T.op('act', lambda e: e.activation(out=V[:, 4 + tb, p * 512:(p + 1) * 512], in_=vst[:, vi, :], func=AF.Identity), [('vst', vi)], ['Vn'])
                        T.dma('sp', v_out[j, tb * 128:(tb + 1) * 128, p * 512:(p + 1) * 512], vst[:, vi, :], [('vst', vi)], [], ('o_vst', vi))
                    if i == 0:
                        mod_piece(8 + p)
                if i == 0:
                    mod_piece(10)
                    mod_piece(11)
                    mod_finish(0, 'B')
                T.muted = False
                items = [(h, qt, kb) for h in range(0 if 'core' in SKIP else 8) for qt in range(4) for kb in range(12)]
                T.op('dve', lambda e: e.memset(Qp[:], 0.0), [], [('Qp', 0), ('Qp', 1)])
                def emit_score(n):
                    h, qt, kb = items[n]
                    sbk = n % 3
                    slot = (h * 4 + qt) % 2
                    qs = slice(qt * 256, (qt + 1) * 256)
                    ks = slice(kb * 128, (kb + 1) * 128)
                    if kb == 0:
                        T.op('dve', lambda e: e.tensor_copy(out=Qp[0:64, slot, 0:256], in_=QT[0:64, h, qs]), ['QT'], [('Qp', slot)])
                        T.op('dve', lambda e: e.tensor_copy(out=Qp[64:128, slot, 256:512], in_=QT[64:128, h, qs]), ['QT'], [('Qp', slot)])
                    T.op('pe', lambda e: e.matmul(PS(sbk), KT[:, h, ks], Qp[:, slot, :], start=True, stop=True),
                         ['KTc', 'KTn', ('Qp', slot)], [('ps', sbk)], signal=True)
                for n0 in range(min(2, len(items))):
                    emit_score(n0)
                for n, (h, qt, kb) in enumerate(items):
                    if n + 2 < len(items):
                        emit_score(n + 2)
                    sbk = n % 3
                    pi = n % 3
                    qs = slice(qt * 256, (qt + 1) * 256)
                    T.op('act', lambda e: e.activation(out=Pt[:, pi, :], in_=PS(sbk), func=AF.Exp,
                                                       bias=vecs[:, C_MASK + kb * 4 + qt:C_MASK + kb * 4 + qt + 1], scale=0.125),
                         [('ps', sbk), 'vecs'], [('Pt', pi)])
                    T.op('pe', lambda e: e.matmul(PS(4), V[:, kb, h * 128:(h + 1) * 128], Pt[:, pi, :], start=(kb == 0), stop=(kb == 11)),
                         ['Vc', 'Vn', ('Pt', pi)], [('ps', 4)], signal=False)
                    T.op('pe', lambda e: e.matmul(PS(5), ones1[:], Pt[:, pi, :], start=(kb == 0), stop=(kb == 11)),
                         ['ones1', ('Pt', pi)], [('ps', 5)], signal=True)
                    if kb != 11:
                        continue
                    T.op('dve', lambda e: e.tensor_copy(out=tt[:], in_=PS(4)), [('ps', 4)], ['tt'])
                    T.op('dve', lambda e: e.tensor_copy(out=rr[:], in_=PS(5)), [('ps', 5)], ['rr'])
                    T.op('dve', lambda e: e.reciprocal(out=rr[:], in_=rr[:]), ['rr'], ['rr'])
                    T.op('dve', lambda e: e.tensor_tensor(out=tt[:], in0=tt[:], in1=rr[:], op=ALU.mult), ['tt', 'rr'], ['tt'])
                    T.op('dve', lambda e: e.scalar_tensor_tensor(out=Oh[:, 0, qs], in0=tt[:, 256:512], scalar=nlam, in1=tt[:, 0:256],
                                                                 op0=ALU.mult, op1=ALU.add), ['tt', 'small'], [('Oh', 0)])
                    if qt != 3:
                        continue
                    for th in range(2):
                        T.op('act', lambda e: e.activation(out=sqo[:, th, :], in_=Oh[:, 0, half(th)], func=AF.Square), [('Oh', 0)], [('sqo', th)])
                        T.op('pe', lambda e: e.matmul(PS(6), ones128[:], sqo[:, th, :], start=True, stop=True), ['ones128', ('sqo', th)], [('ps', 6)])
                        T.op('act', lambda e: e.activation(out=lnt[:], in_=PS(6), func=AF.Ln, bias=EPS, scale=1.0), [('ps', 6)], ['lnt'])
                        T.op('act', lambda e: e.activation(out=rstd[:], in_=lnt[:], func=AF.Exp, scale=-0.5), ['lnt'], ['rstd'])
                        T.op('dve', lambda e: e.scalar_tensor_tensor(out=hT[:, h, half(th)], in0=Oh[:, 0, half(th)], scalar=gsc, in1=rstd[:],
                                                                     op0=ALU.mult, op1=ALU.mult), [('Oh', 0), 'small', 'rstd'], [('hT', th)])
                T.muted = 'wo' in SKIP
                out_proj_residual(hT, 16, 'hT')
                T.muted = False
                T.barrier()

        def hgrn(i, j):
            with contextlib.ExitStack() as ph:
                ON = sb("ON", [128, 8, NTOK], BF16, ph)
                qh = sb("qh", [128, NTOK], F32, ph)
                gateh = sb("gateh", [128, NTOK], F32, ph)
                TH = sb("TH", [128, 2, NTOK], F32, ph)
                Fb = sb("Fb", [128, NTOK], F32, ph)
                Bc = sb("Bc", [128, NTOK], F32, ph)
                E1 = sb("E1", [128, NTOK], F32, ph)
                E2 = sb("E2", [128, NTOK], F32, ph)
                Dd = sb("Dd", [128, 2, 64], F32, ph)
                Qt = sb("Qt", [128, 2, NTOK], BF16, ph)
                Kt = sb("Kt", [128, NTOK], BF16, ph)
                Sb = sb("Sb", [128, 2, 3, 128], BF16, ph)
                Kh = sb("Kh", [128, NTOK], BF16, ph)
                Ktm = sb("Ktm", [128, 2, 8, 128], BF16, ph)
                Vh = sb("Vh", [128, 8, 128], BF16, ph)
                Vexp = sb("Vexp", [128, 8, 8, 128], BF16, ph)
                cmk = sb("cmk", [128, 8, 128], BF16, ph)
                hmk = sb("hmk", [128, 2, 128], F32, ph)
                rmask = sb("rmask", [128, 1, NTOK], F32, ph)
                Mf = sb("Mf", [128, 2, NTOK], BF16, ph)
                Mb = sb("Mb", [128, NTOK], BF16, ph)
                Sd = sb("Sd", [128, 2, 2, 128], F32, ph)
                S0 = sb("S0", [128, 2, 128], F32, ph)
                stage = sb("stage", [128, 4, 2, 128], F32, ph)
                lbt = sb("lbt", [128, 64], F32, ph)
                lbs = sb("lbs", [128, 80], F32, ph)
                sqo = sb("sqo2", [128, 512], BF16, ph)
                T.dma('pool', cmk[:], cmask_in, [], ['cmk'], 'ld_cmk')
                T.dma('sp', hmk[:], hmask_in, [], ['hmk'], 'ld_hmk')
                r4 = rmask[:].rearrange("p d (c s) -> p d c s", s=16)
                T.op('dve', lambda e: e.memset(rmask[:], 1.0), [], ['rmask'])
                T.op('dve', lambda e: e.memset(r4[:, 0, :, 0:1], 0.0), [], ['rmask'])
                T.op('act', lambda e: e.activation(out=lbt[:], in_=vecs[:, C_LBL:C_LBL + 64], func=AF.Exp), ['vecs'], ['lbt'])
                l4 = lbt[:].rearrange("p (d l h) -> p d l h", d=2, l=4)
                den = lbs[:, 0:16].rearrange("p (d h) -> p d h", d=2)
                num = lbs[:, 16:32].rearrange("p (d h) -> p d h", d=2)
                lb = lbs[:, 32:48].rearrange("p (d h) -> p d h", d=2)
                c0 = lbs[:, 48:64].rearrange("p (d h) -> p d h", d=2)
                c1 = lbs[:, 64:80].rearrange("p (d h) -> p d h", d=2)
                T.op('dve', lambda e: e.tensor_tensor(out=den, in0=l4[:, :, 0, :], in1=l4[:, :, 1, :], op=ALU.add), ['lbt'], ['lbs'])
                T.op('dve', lambda e: e.tensor_tensor(out=den, in0=den, in1=l4[:, :, 2, :], op=ALU.add), ['lbt', 'lbs'], ['lbs'])
                T.op('dve', lambda e: e.tensor_tensor(out=den, in0=den, in1=l4[:, :, 3, :], op=ALU.add), ['lbt', 'lbs'], ['lbs'])
                T.op('dve', lambda e: e.tensor_copy(out=num, in_=l4[:, :, 1, :]), ['lbt', 'lbs'], ['lbs'])
                for l in range(2, i + 1):
                    T.op('dve', lambda e: e.tensor_tensor(out=num, in0=num, in1=l4[:, :, l, :], op=ALU.add), ['lbt', 'lbs'], ['lbs'])
                T.op('dve', lambda e: e.reciprocal(out=den, in_=den), ['lbs'], ['lbs'])
                T.op('dve', lambda e: e.tensor_tensor(out=lb, in0=num, in1=den, op=ALU.mult), ['lbs'], ['lbs'])
                T.op('dve', lambda e: e.tensor_scalar(out=c0, in0=lb, scalar1=0.5, scalar2=0.5, op0=ALU.mult, op1=ALU.add), ['lbs'], ['lbs'])
                T.op('dve', lambda e: e.tensor_scalar(out=c1, in0=lb, scalar1=-0.5, scalar2=0.5, op0=ALU.mult, op1=ALU.add), ['lbs'], ['lbs'])
                nc1 = small[:, 8:10]

                adanorm(A1, modv[:, 0:8])
                for h in range(8):
                    s, sl = ws_get('rec')
                    wv = lambda g: sl[:, g * 1024:(g + 1) * 1024].rearrange("p (k c) -> p k c", c=128)
                    T.dma('sp', S0[:], s0_in[h], [], ['S0'], 'ld_s0')
                    cnt = 0
                    for (g, kind_) in ((0, 'q'), (2, 'g'), (3, 'zf'), (4, 'zb')):
                        w = wv(g)
                        for th in range(2):
                            b = 2 + (cnt % 2)
                            cnt += 1
                            for k in range(8):
                                T.op('pe', lambda e: e.matmul(PS(b), w[:, k, :], hT[:, k, half(th)], start=(k == 0), stop=(k == 7)),
                                     [('w', s), ('hT', th)], [('ps', b)], signal=(k == 7))
                            if kind_ == 'q':
                                T.op('act', lambda e: e.activation(out=qh[:, half(th)], in_=PS(b), func=AF.Silu), [('ps', b)], ['qh'])
                            elif kind_ == 'g':
                                T.op('act', lambda e: e.activation(out=gateh[:, half(th)], in_=PS(b), func=AF.Silu), [('ps', b)], ['gateh'])
                            else:
                                d = 0 if kind_ == 'zf' else 1
                                T.op('act', lambda e: e.activation(out=TH[:, d, half(th)], in_=PS(b), func=AF.Tanh, scale=0.5), [('ps', b)], [('TH', d)])
                    w = wv(1)
                    for tq in range(2):
                        b = 2 + (cnt % 2)
                        cnt += 1
                        for t4 in range(4):
                            tb = tq * 4 + t4
                            for k in range(8):
                                T.op('pe', lambda e: e.matmul(ps[:, b, t4 * 128:(t4 + 1) * 128], hT[:, k, tb * 128:(tb + 1) * 128], w[:, k, :],
                                                              start=(k == 0), stop=(k == 7)),
                                     [('w', s), ('hT', tb // 4)], [('ps', b)], signal=(k == 7 and t4 == 3))
                        T.op('act', lambda e: e.activation(out=Vh[:, tq * 4:(tq + 1) * 4, :], in_=PS(b).rearrange("p (t v) -> p t v", t=4), func=AF.Identity),
                             [('ps', b)], ['Vh'])
                    for tb in range(8):
                        T.op('pool', lambda e: e.tensor_tensor(out=Vexp[:, tb, :, :], in0=Vh[:, tb:tb + 1, :].to_broadcast([128, 8, 128]), in1=cmk[:], op=ALU.mult),
                             ['Vh', 'cmk'], ['Vexp'])
                    for d in range(2):
                        c0d, c1d = c0[:, d, h:h + 1], c1[:, d, h:h + 1]
                        T.op('dve', lambda e: e.tensor_scalar(out=nc1[:, d:d + 1], in0=c1d, scalar1=-1.0, scalar2=None, op0=ALU.mult), ['lbs'], [('nc1', d)])
                        T.op('act', lambda e: e.activation(out=Fb[:], in_=TH[:, d, :], func=AF.Identity, scale=c1d, bias=c0d),
                             [('TH', d), 'lbs'], ['Fb'])
                        T.op('act', lambda e: e.activation(out=TH[:, d, :], in_=TH[:, d, :], func=AF.Identity, scale=nc1[:, d:d + 1], bias=c1d),
                             [('TH', d), 'lbs', ('nc1', d)], [('TH', d)])
                        T.op('act', lambda e: e.activation(out=Fb[:], in_=Fb[:], func=AF.Ln), ['Fb'], ['Fb'])
                        if d == 0:
                            T.op('dve', lambda e: e.tensor_tensor_scan(out=Bc[:], data0=rmask[:, 0, :], data1=Fb[:], initial=0.0, op0=ALU.mult, op1=ALU.add),
                                 ['Fb', 'rmask'], ['Bc'])
                        else:
                            T.op('dve', lambda e: e.tensor_tensor_scan(out=Bc[:, ::-1], data0=rmask[:, 0, :], data1=Fb[:, ::-1], initial=0.0,
                                                                       op0=ALU.mult, op1=ALU.add), ['Fb', 'rmask'], ['Bc'])
                        T.op('act', lambda e: e.activation(out=E1[:], in_=Bc[:], func=AF.Exp), ['Bc'], ['E1'])
                        T.op('act', lambda e: e.activation(out=E2[:], in_=Bc[:], func=AF.Exp, scale=-1.0), ['Bc'], ['E2'])
                        e14 = E1[:].rearrange("p (c s) -> p c s", s=16)
                        lastpos = 15 if d == 0 else 0
                        T.op('dve', lambda e: e.tensor_copy(out=Dd[:, d, :], in_=e14[:, :, lastpos]), ['E1'], [('Dd', d)])
                        T.op('dve', lambda e: e.tensor_tensor(out=Qt[:, d, :], in0=qh[:], in1=E1[:], op=ALU.mult), ['qh', 'E1'], [('Qt', d)])
                        T.op('dve', lambda e: e.tensor_tensor(out=Bc[:], in0=TH[:, d, :], in1=E2[:], op=ALU.mult), [('TH', d), 'E2', 'Bc'], ['Bc'])
                        T.op('act', lambda e: e.activation(out=Kt[:], in_=Bc[:], func=AF.Identity), ['Bc'], ['Kt'])
                        T.op('dve', lambda e: e.tensor_tensor(out=Kh[:].rearrange("p (c s) -> p c s", s=16), in0=Bc[:].rearrange("p (c s) -> p c s", s=16),
                                                              in1=Dd[:, d, :].unsqueeze(2).to_broadcast([128, 64, 16]), op=ALU.mult),
                             ['Bc', ('Dd', d)], ['Kh'])
                        for tb in range(8):
                            T.op('pe', lambda e: e.transpose(pbf[:, tb * 128:(tb + 1) * 128], Kh[:, tb * 128:(tb + 1) * 128], cstb[:, 1, :]),
                                 ['Kh', 'cstb'], ['pbf'], signal=(tb == 7))
                        T.op('act', lambda e: e.activation(out=Ktm[:, d, :, :], in_=pbf[:].rearrange("p (t k) -> p t k", t=8), func=AF.Identity), ['pbf'], [('Ktm', d)])
                        for tq in range(2):
                            b = 2 + (cnt % 2)
                            cnt += 1
                            for t4 in range(4):
                                tb = tq * 4 + t4
                                bs = slice(tb * 128, (tb + 1) * 128)
                                T.op('pe', lambda e: e.matmul(ps[:, b, t4 * 128:(t4 + 1) * 128], Kt[:, bs], Qt[:, d, bs], start=True, stop=True),
                                     ['Kt', ('Qt', d)], [('ps', b)], signal=(t4 == 3))
                            T.op('dve', lambda e: e.tensor_tensor(out=Mf[:, d, tq * 512:(tq + 1) * 512].rearrange("p (t k) -> p t k", t=4),
                                                                  in0=PS(b).rearrange("p (t k) -> p t k", t=4),
                                                                  in1=hmk[:, d:d + 1, :].to_broadcast([128, 4, 128]), op=ALU.mult),
                                 [('ps', b), 'hmk'], [('Mf', d)])
                    T.op('dve', lambda e: e.tensor_tensor(out=Mb[:], in0=Mf[:, 0, :], in1=Mf[:, 1, :], op=ALU.add), [('Mf', 0), ('Mf', 1)], ['Mb'])
                    for bk in range(2):
                        T.op('pe', lambda e: e.matmul(PS(bk), zerosW[:], Mb[:, 0:512], start=True, stop=False, skip_group_check=True),
                             ['zerosW', 'Mb'], [('ps', bk)], signal=False)
                    for tb in range(8):
                        T.op('pe', lambda e: e.matmul(ps[:, tb // 4, (tb % 4) * 128:(tb % 4 + 1) * 128], Vh[:, tb, :], Mb[:, tb * 128:(tb + 1) * 128],
                                                      start=False, stop=False, skip_group_check=True),
                             ['Vh', 'Mb'], [('ps', tb // 4)], signal=(tb % 4 == 3))
                    T.op('dve', lambda e: e.tensor_copy(out=Sd[:, :, 0, :], in_=S0[:]), ['S0'], [('S', 0, 0), ('S', 1, 0)])
                    cur = [0, 0]
                    kvbank = lambda d, g: 3 + 2 * d + (g % 2)
                    def emit_kv(d, g):
                        tb, hf = g // 2, g % 2
                        b = kvbank(d, g)
                        T.op('pe', lambda e: e.matmul(PS(b), Ktm[:, d, tb, :], Vexp[:, tb, hf * 4:(hf + 1) * 4, :].rearrange("p c v -> p (c v)"),
                                                      start=True, stop=True),
                             [('Ktm', d), 'Vexp'], [('ps', b)])
                    emit_kv(0, 0)
                    emit_kv(1, 15)
                    for step in range(64):
                        for d in range(2):
                            c = step if d == 0 else 63 - step
                            g = c // 4
                            first_in_group = (c % 4 == 0) if d == 0 else (c % 4 == 3)
                            if first_in_group:
                                gn = g + 1 if d == 0 else g - 1
                                if 0 <= gn < 16:
                                    emit_kv(d, gn)
                            seq_start = (c % 16 == 0 and c > 0) if d == 0 else (c % 16 == 15 and c < 63)
                            cu, nx = cur[d], 1 - cur[d]
                            if seq_start:
                                T.op('dve', lambda e: e.tensor_scalar(out=Sd[:, d, cu, :], in0=Sd[:, d, cu, :], scalar1=vecs[:, C_KEEP:C_KEEP + 1], scalar2=None, op0=ALU.mult),
                                     [('S', d, cu), 'vecs'], [('S', d, cu)])
                            si = step % 3
                            if False:
                                T.op('pool', lambda e: e.tensor_copy(out=Sb[:, d, si, :], in_=Sd[:, d, cu, :]), [('S', d, cu)], [('Sb', d, si)])
                            else:
                                T.op('act', lambda e: e.activation(out=Sb[:, d, si, :], in_=Sd[:, d, cu, :], func=AF.Identity), [('S', d, cu)], [('Sb', d, si)])
                            T.op('pe', lambda e: e.matmul(ps[:, c // 32, (c % 32) * 16:(c % 32 + 1) * 16], Sb[:, d, si, :], Qt[:, d, c * 16:(c + 1) * 16],
                                                          start=False, stop=(step == 63), skip_group_check=True),
                                 [('Sb', d, si), ('Qt', d)], [('ps', c // 32)])
                            b = kvbank(d, g)
                            T.op('dve', lambda e: e.scalar_tensor_tensor(out=Sd[:, d, nx, :], in0=Sd[:, d, cu, :], scalar=Dd[:, d, c:c + 1],
                                                                         in1=ps[:, b, (c % 4) * 128:(c % 4 + 1) * 128], op0=ALU.mult, op1=ALU.add),
                                 [('S', d, cu), ('Dd', d), ('ps', b)], [('S', d, nx)])
                            seq_end = (c % 16 == 15) if d == 0 else (c % 16 == 0)
                            if seq_end:
                                T.op('dve', lambda e: e.tensor_copy(out=stage[:, c // 16, d, :], in_=Sd[:, d, nx, :]), [('S', d, nx)], ['stage'])
                            cur[d] = nx
                    T.dma('sp', s_out[:, :, h].rearrange("s d k v -> k s d v"), stage[:], ['stage'], [], 'o_stage')
                    for th in range(2):
                        T.op('act', lambda e: e.activation(out=sqo[:], in_=PS(th), func=AF.Square), [('ps', th)], ['sqo'])
                        T.op('pe', lambda e: e.matmul(PS(2), ones128[:], sqo[:], start=True, stop=True), ['ones128', 'sqo'], [('ps', 2)])
                        T.op('act', lambda e: e.activation(out=lnt[:], in_=PS(2), func=AF.Ln, bias=EPS, scale=1.0), [('ps', 2)], ['lnt'])
                        T.op('act', lambda e: e.activation(out=rstd[:], in_=lnt[:], func=AF.Exp, scale=-0.5), ['lnt'], ['rstd'])
                        T.op('dve', lambda e: e.scalar_tensor_tensor(out=tmpA[:, 0, :], in0=PS(th), scalar=vecs[:, C_GOUT:C_GOUT + 1], in1=rstd[:],
                                                                     op0=ALU.mult, op1=ALU.mult), [('ps', th), 'vecs', 'rstd'], [('tmpA', 0)])
                        T.op('dve', lambda e: e.tensor_tensor(out=ON[:, h, half(th)], in0=tmpA[:, 0, :], in1=gateh[:, half(th)], op=ALU.mult),
                             [('tmpA', 0), 'gateh'], [('ON', th)])
                out_proj_residual(ON, 16, 'ON')
                T.barrier()

        def fourier(i, j):
            with contextlib.ExitStack() as ph:
                CN = sb("CN", [128, 2, 8, NTOK], BF16, ph)
                CS = sb("CS", [128, 2, 512], BF16, ph)
                Af = sb("Af", [128, 8, 4, 512], BF16, ph)
                for a in range(2):
                    T.dma('pool', CN[:, a, :, :], dftn_in[a].rearrange("(kb p) n -> p kb n", p=128), [], ['CN'], 'ld_cn')
                T.dma('pool', CS[:], dftc_in.rearrange("(cc p) n -> p cc n", p=128), [], ['CS'], 'ld_cs')
                adanorm(A1, modv[:, 0:8])
                cnt = 0
                for tb in range(8):
                    for g in range(4):
                        b = 2 + (cnt % 2)
                        cnt += 1
                        for cc in range(2):
                            T.op('pe', lambda e: e.matmul(PS(b), hT[:, g * 2 + cc, tb * 128:(tb + 1) * 128], CS[:, cc, :], start=(cc == 0), stop=(cc == 1)),
                                 [('hT', tb // 4), 'CS'], [('ps', b)], signal=(cc == 1))
                        if cnt % 2:
                            T.op('act', lambda e: e.activation(out=Af[:, tb, g, :], in_=PS(b), func=AF.Identity), [('ps', b)], ['Af'])
                        else:
                            T.op('dve', lambda e: e.tensor_copy(out=Af[:, tb, g, :], in_=PS(b)), [('ps', b)], ['Af'])
                for fc in range(8):
                    g, cq = fc // 2, fc % 2
                    for th in range(2):
                        b = 2 + (cnt % 2)
                        cnt += 1
                        n = 0
                        for tb in range(8):
                            for a in range(2):
                                T.op('pe', lambda e: e.matmul(PS(b), Af[:, tb, g, a * 256 + cq * 128: a * 256 + (cq + 1) * 128], CN[:, a, tb, half(th)],
                                                              start=(n == 0), stop=(n == 15)),
                                     ['Af', 'CN'], [('ps', b)], signal=(n == 15))
                                n += 1
                        T.op('act', lambda e: e.activation(out=hT[:, fc, half(th)], in_=PS(b), func=AF.Identity), [('ps', b)], [('hT', th)])
                out_proj_residual(hT, 16, 'hT')
                T.barrier()

        for p in range(4):
            mod_piece(p)
        for i in range(depth_run):
            kind, j = i % 3, i // 3
            mod_finish(i, 'A' if i == 0 else 'all')
            if 'mixer' in SKIP:
                for _ in range(16 if kind == 1 else (8 if kind == 0 else 2)):
                    ws['next'] += 1
                ffn(i)
                continue
            if kind == 0:
                attention(i, j)
            elif kind == 1:
                hgrn(i, j)
            else:
                fourier(i, j)
            ffn(i)
        if True:
            gfin = vecs[:, C_GFIN:C_GFIN + 8]
            for th in range(2):
                def out_fn(k, ti, th=th):
                    T.dma('sp', y_out[k * 128:(k + 1) * 128, half(th)], tmpA[:, ti, :], [('tmpA', ti)], [], ('o_y', ti))
                norm_apply(th, gfin, None, out_fn)
        for key in list(T.sems.keys()):
            if isinstance(key, tuple) and isinstance(key[0], str) and key[0].startswith('o_') or key == 'o_stage':
                nc.sync.wait_ge(T.sems[key], T.count[key])
        T.barrier()
    return nc


def _host_tables(is_sample):
    p = np.arange(128)
    d = p % 64
    t = np.arange(NTOK)
    if is_sample:
        row, col = t // 64, t % 64
        dd = d % 32
        idx = dd % 16
        inv = 10000.0 ** (-(idx.astype(np.float64)) / 16.0)
        pos = np.where((d < 32)[:, None], row[None, :], col[None, :]).astype(np.float64)
        ang = pos * inv[:, None]
        cos = np.cos(ang)
        sin = np.sin(ang)
        sin = np.where((dd < 16)[:, None], -sin, sin)
    else:
        cos = np.ones((128, NTOK))
        sin = np.zeros((128, NTOK))
    rope = np.stack([cos, sin]).astype(np.float32)
    if is_sample:
        n = np.arange(NTOK, dtype=np.float64)
        ang = 2 * np.pi * np.outer(n, n) / NTOK
        norm = 1.0 / math.sqrt(NTOK * 256.0)
        cn, sn = np.cos(ang) * norm, -np.sin(ang) * norm
    else:
        n = np.arange(256, dtype=np.float64)
        ang = 2 * np.pi * np.outer(n, n) / 256.0
        norm = 1.0 / 256.0
        cn = np.zeros((NTOK, NTOK)); sn = np.zeros((NTOK, NTOK))
        for s in range(4):
            cn[s * 256:(s + 1) * 256, s * 256:(s + 1) * 256] = np.cos(ang) * norm
            sn[s * 256:(s + 1) * 256, s * 256:(s + 1) * 256] = -np.sin(ang) * norm
    dftn = np.stack([cn, sn]).astype(np.float32)
    c = np.arange(256, dtype=np.float64)
    angc = 2 * np.pi * np.outer(c, c) / 256.0
    dftc = np.concatenate([np.cos(angc), np.sin(angc)], axis=1).astype(np.float32)
    return rope, dftn, dftc


def _const_tables():
    R = np.zeros((128, 128), np.float32)
    for p in range(128):
        partner = p + 16 if (p % 32) < 16 else p - 16
        R[partner, p] = 1.0
    cst = np.stack([R, np.eye(128, dtype=np.float32)])
    s = np.arange(128)
    same = (s[:, None] // 16) == (s[None, :] // 16)
    hmask = np.stack([same & (s[:, None] <= s[None, :]), same & (s[:, None] >= s[None, :])], axis=1).astype(np.float32)
    cmask = np.zeros((128, 8, 128), np.float32)
    for cc in range(8):
        cmask[cc * 16:(cc + 1) * 16, cc, :] = 1.0
    return cst, hmask, cmask


def _pack_vecs(inp, cond, is_sample):
    v = np.zeros((128, NV), np.float32)
    fm = lambda a: np.asarray(a, np.float32).reshape(-1, 128).T
    v[:, C_COND:C_COND + 8] = fm(cond)
    for i in range(DEPTH):
        v[:, C_GMIX + i * 8:C_GMIX + (i + 1) * 8] = fm(inp['g_norm_mix'][i])
        v[:, C_GFFN + i * 8:C_GFFN + (i + 1) * 8] = fm(inp['g_norm_ffn'][i])
        v[:, C_BADA + i * 48:C_BADA + (i + 1) * 48] = fm(inp['b_ada'][i])
    v[:, C_GFIN:C_GFIN + 8] = fm(inp['g_final'])
    for j in range(2):
        v[:, C_GSUB + j] = inp['g_subln_attn'][j]
    v[:, C_GOUT] = inp['g_out_rec'][0]
    lbl = np.asarray(inp['lb_logits_rec'], np.float32).reshape(2, 4, 8, 128)
    v[:, C_LBL:C_LBL + 64] = lbl.transpose(3, 0, 1, 2).reshape(128, 64)
    if not is_sample:
        m = np.full((12, 4), -30000.0, np.float32)
        for kb in range(4, 12):
            m[kb, (kb - 4) // 2] = 0.0
        v[:, C_MASK:C_MASK + 48] = m.reshape(1, 48)
    v[:, C_KEEP] = 1.0 if is_sample else 0.0
    v[:, C_LAM:C_LAM + 512] = np.asarray(inp['lam_attn'], np.float32).reshape(1, 512)
    return v


_CACHE = {}


def kernel(**inp):
    inp = {k: np.asarray(v) for k, v in inp.items()}
    if 'nc' not in _CACHE:
        _CACHE['nc'] = build_program(int(os.environ.get('KDEPTH', DEPTH)))
    nc = _CACHE['nc']
    cst, hmask, cmask = _const_tables()
    tabs = {True: _host_tables(True), False: _host_tables(False)}
    shared = {
        'w_ada': inp['w_ada'], 'w_qkv': inp['w_qkv_attn'], 'w_oa': inp['w_o_attn'], 'w_rin': inp['w_in_rec'],
        'w_or': inp['w_o_rec'], 'w_four': inp['w_four'], 'w_fin': inp['w_ffn_in'], 'w_fout': inp['w_ffn_out'],
        'cst': cst, 'hmask': hmask, 'cmask': cmask,
    }
    shared = {k: np.ascontiguousarray(v, dtype=np.float32) for k, v in shared.items()}
    in_maps = []
    for r in range(8):
        is_sample = r < 4
        rope, dftn, dftc = tabs[is_sample]
        m = dict(shared)
        if is_sample:
            b = r
            x = inp['x_sample'][b]
            m['kcT'] = np.ascontiguousarray(inp['cache_attn_k'][b].reshape(2, 512, D).transpose(0, 2, 1))
            m['vc'] = np.ascontiguousarray(inp['cache_attn_v'][b].reshape(2, 512, D))
            m['s0'] = np.ascontiguousarray(inp['state_hgrn'][b, 0].transpose(1, 2, 0, 3))
            cond = inp['c'][b]
        else:
            q = r - 4
            x = inp['x_prompt'][4 * q:4 * q + 4].reshape(NTOK, D)
            m['kcT'] = np.zeros((2, D, 512), np.float32)
            m['vc'] = np.zeros((2, 512, D), np.float32)
            m['s0'] = np.zeros((8, 128, 2, 128), np.float32)
            cond = inp['c_ctx']
        m['xT'] = np.ascontiguousarray(x.T.astype(np.float32))
        m['vecs'] = _pack_vecs(inp, cond, is_sample)
        m['rope'], m['dftn'], m['dftc'] = rope, dftn, dftc
        in_maps.append(m)
    res = run_bass_kernel_spmd(nc, in_maps[:NCORES], core_ids=list(range(NCORES)))
    outs = res.results
    if NCORES < 8:
        return outs
    y_sample = np.stack([outs[r]['yT'].T for r in range(4)]).astype(np.float32)
    y_prompt = np.concatenate([outs[r]['yT'].T.reshape(4, 256, D) for r in range(4, 8)]).astype(np.float32)
    nk = np.concatenate([outs[r]['koutT'].transpose(2, 0, 1).reshape(4, 256, 2, D).transpose(0, 2, 1, 3) for r in range(4, 8)])
    new_k = np.ascontiguousarray(nk.reshape(16, 2, 256, 8, 2, 64)).astype(np.float32)
    nv = np.concatenate([outs[r]['vout'].reshape(2, 4, 256, D).transpose(1, 0, 2, 3) for r in range(4, 8)])
    new_v = np.ascontiguousarray(nv.reshape(16, 2, 256, 8, 128)).astype(np.float32)
    ns = np.concatenate([outs[r]['sout'] for r in range(4, 8)])
    new_s = np.ascontiguousarray(ns.reshape(16, 1, 2, 8, 128, 128)).astype(np.float32)
    return (np.ascontiguousarray(y_prompt), np.ascontiguousarray(y_sample), new_k, new_v, new_s)
```
